# Optimizing a Trainium2 kernel written in Bass

```python
import jax, jax.numpy as jnp
from jax import lax
import numpy as np

D_MODEL = 1024
BATCH = 2
SEQ = 8192
DEPTH = 1

GRID_W = 64
HEAD_DIM = 64
ATTN_WIDTH = D_MODEL // 2
N_HEADS = ATTN_WIDTH // HEAD_DIM
N_KV_HEADS = max(1, N_HEADS // 4)
KV_WIDTH = N_KV_HEADS * HEAD_DIM
CONV_CH = D_MODEL - ATTN_WIDTH
CONV_GROUPS = 8
CONV_GROUP_DIM = CONV_CH // CONV_GROUPS
MIX_WIDTH = ATTN_WIDTH + CONV_CH
IN_COLS = ATTN_WIDTH + 2 * KV_WIDTH + 2 * CONV_CH
CONV_WIDTH = 31
FFN_CONV_WIDTH = 3
D_FF = (11 * D_MODEL // 4 + 127) // 128 * 128
ROPE_THETA = 10000.0
ROPE_FREQS = HEAD_DIM // 4
Q_BLOCK = 128
EPS = 1e-6
ALPHA = (2.0 * DEPTH) ** 0.25
BETA = (8.0 * DEPTH) ** -0.25

kernel_name = "hybrid_attn_conformer_convffn_deepnorm_adaln"


def layer_norm(x, g, b):
    xf = x.astype(jnp.float32)
    mu = jnp.mean(xf, axis=-1, keepdims=True)
    var = jnp.mean(jnp.square(xf - mu), axis=-1, keepdims=True)
    y = (xf - mu) * lax.rsqrt(var + EPS)
    return (y * g.astype(jnp.float32) + b.astype(jnp.float32)).astype(x.dtype)


def adaln_modulate(x, shift, scale):
    xf = x.astype(jnp.float32)
    mu = jnp.mean(xf, axis=-1, keepdims=True)
    var = jnp.mean(jnp.square(xf - mu), axis=-1, keepdims=True)
    y = ((xf - mu) * lax.rsqrt(var + EPS)).astype(x.dtype)
    return y * (1.0 + scale) + shift


def rms_norm(x, g):
    xf = x.astype(jnp.float32)
    y = xf * lax.rsqrt(jnp.mean(jnp.square(xf), axis=-1, keepdims=True) + EPS)
    return (y * g.astype(jnp.float32)).astype(x.dtype)


def depthwise_conv(x, w, b):
    width, ch = w.shape
    pad = (width - 1) // 2
    y = lax.conv_general_dilated(
        x, w.astype(x.dtype)[:, None, :], window_strides=(1,), padding=[(pad, pad)],
        dimension_numbers=('NWC', 'WIO', 'NWC'), feature_group_count=ch)
    return y + b.astype(x.dtype)


def axial_rope_tables(seq_len, dtype):
    n_rows = seq_len // GRID_W
    rows = jnp.repeat(jnp.arange(n_rows, dtype=jnp.float32), GRID_W)
    cols = jnp.tile(jnp.arange(GRID_W, dtype=jnp.float32), n_rows)
    inv_freq = ROPE_THETA ** (-jnp.arange(ROPE_FREQS, dtype=jnp.float32) / ROPE_FREQS)
    ang_r = rows[:, None, None] * inv_freq
    ang_c = cols[:, None, None] * inv_freq
    return (jnp.cos(ang_r).astype(dtype), jnp.sin(ang_r).astype(dtype),
            jnp.cos(ang_c).astype(dtype), jnp.sin(ang_c).astype(dtype))


def rotate(x, cos, sin):
    x1, x2 = jnp.split(x, 2, axis=-1)
    return jnp.concatenate([x1 * cos - x2 * sin, x2 * cos + x1 * sin], axis=-1)


def apply_axial_rope(x, tables):
    cos_r, sin_r, cos_c, sin_c = tables
    x_row, x_col = jnp.split(x, 2, axis=-1)
    return jnp.concatenate([rotate(x_row, cos_r, sin_r), rotate(x_col, cos_c, sin_c)], axis=-1)


def gqa_attention(q, k, v):
    b, s = q.shape[0], q.shape[1]
    groups = N_HEADS // N_KV_HEADS
    scale = HEAD_DIM ** -0.5
    qb = q.reshape(b, s // Q_BLOCK, Q_BLOCK, N_KV_HEADS, groups, HEAD_DIM)
    qb = jnp.moveaxis(qb, 1, 0)

    def one_block(q_blk):
        scores = jnp.einsum('bqhgd,bkhd->bhgqk', q_blk, k,
                            preferred_element_type=jnp.float32) * scale
        probs = jax.nn.softmax(scores, axis=-1).astype(v.dtype)
        return jnp.einsum('bhgqk,bkhd->bqhgd', probs, v)

    out = lax.map(one_block, qb)
    return jnp.moveaxis(out, 0, 1).reshape(b, s, N_HEADS, HEAD_DIM)


def setup_inputs(seed: int = 0) -> dict:
    key = jax.random.key(seed)
    ks = jax.random.split(key, 24)
    f32 = jnp.float32

    def nrm(k, shape, scale):
        return jax.random.normal(k, shape, f32) * scale

    def gain(k, shape):
        return 1.0 + 0.02 * jax.random.normal(k, shape, f32)

    x = jax.random.normal(ks[0], (BATCH, SEQ, D_MODEL), f32)
    c = jax.random.normal(ks[1], (BATCH, D_MODEL), f32)
    w_ada = nrm(ks[2], (DEPTH, D_MODEL, 6 * D_MODEL), 0.5 * D_MODEL ** -0.5)
    b_ada = nrm(ks[3], (DEPTH, 6 * D_MODEL), 0.01)
    col_scale = jnp.concatenate([jnp.ones((ATTN_WIDTH + KV_WIDTH,), f32),
                                 jnp.full((KV_WIDTH,), BETA, f32),
                                 jnp.ones((2 * CONV_CH,), f32)])
    w_in = nrm(ks[4], (DEPTH, D_MODEL, IN_COLS), D_MODEL ** -0.5) * col_scale
    q_norm_g = gain(ks[5], (DEPTH, HEAD_DIM))
    k_norm_g = gain(ks[6], (DEPTH, HEAD_DIM))
    conv_dw_w = nrm(ks[7], (DEPTH, CONV_WIDTH, CONV_CH), CONV_WIDTH ** -0.5)
    conv_dw_b = nrm(ks[8], (DEPTH, CONV_CH), 0.02)
    conv_ln_g = gain(ks[9], (DEPTH, CONV_CH))
    conv_ln_b = nrm(ks[10], (DEPTH, CONV_CH), 0.02)
    w_conv_pw2 = nrm(ks[11], (DEPTH, CONV_CH, CONV_CH), BETA * CONV_CH ** -0.5)
    attn_out_g = gain(ks[12], (DEPTH, N_HEADS, HEAD_DIM))
    conv_out_g = gain(ks[13], (DEPTH, CONV_GROUPS, CONV_GROUP_DIM))
    w_o = nrm(ks[14], (DEPTH, MIX_WIDTH, D_MODEL), BETA * MIX_WIDTH ** -0.5)
    ln1_g = gain(ks[15], (DEPTH, D_MODEL))
    ln1_b = nrm(ks[16], (DEPTH, D_MODEL), 0.02)
    w_up = nrm(ks[17], (DEPTH, D_MODEL, 2 * D_FF), D_MODEL ** -0.5)
    ffn_dw_w = nrm(ks[18], (DEPTH, FFN_CONV_WIDTH, 2 * D_FF), FFN_CONV_WIDTH ** -0.5)
    ffn_dw_b = nrm(ks[19], (DEPTH, 2 * D_FF), 0.02)
    w_down = nrm(ks[20], (DEPTH, D_FF, D_MODEL), BETA * D_FF ** -0.5)
    ln2_g = gain(ks[21], (DEPTH, D_MODEL))
    ln2_b = nrm(ks[22], (DEPTH, D_MODEL), 0.02)
    return {"x": x, "c": c, "w_ada": w_ada, "b_ada": b_ada, "w_in": w_in,
            "q_norm_g": q_norm_g, "k_norm_g": k_norm_g,
            "conv_dw_w": conv_dw_w, "conv_dw_b": conv_dw_b,
            "conv_ln_g": conv_ln_g, "conv_ln_b": conv_ln_b, "w_conv_pw2": w_conv_pw2,
            "attn_out_g": attn_out_g, "conv_out_g": conv_out_g, "w_o": w_o,
            "ln1_g": ln1_g, "ln1_b": ln1_b, "w_up": w_up,
            "ffn_dw_w": ffn_dw_w, "ffn_dw_b": ffn_dw_b, "w_down": w_down,
            "ln2_g": ln2_g, "ln2_b": ln2_b}


def reference(x, c, w_ada, b_ada, w_in, q_norm_g, k_norm_g, conv_dw_w, conv_dw_b,
              conv_ln_g, conv_ln_b, w_conv_pw2, attn_out_g, conv_out_g, w_o,
              ln1_g, ln1_b, w_up, ffn_dw_w, ffn_dw_b, w_down, ln2_g, ln2_b):
    b, s, _ = x.shape
    rope = axial_rope_tables(s, x.dtype)
    split_cols = [ATTN_WIDTH, ATTN_WIDTH + KV_WIDTH, ATTN_WIDTH + 2 * KV_WIDTH]
    c_act = jax.nn.silu(c)

    for l in range(DEPTH):
        mod = (c_act @ w_ada[l] + b_ada[l])[:, None, :]
        shift1, scale1, gate1, shift2, scale2, gate2 = jnp.split(mod, 6, axis=-1)

        u = adaln_modulate(x, shift1, scale1)
        proj = u @ w_in[l]
        q, k, v, glu = jnp.split(proj, split_cols, axis=-1)

        q = rms_norm(q.reshape(b, s, N_HEADS, HEAD_DIM), q_norm_g[l])
        k = rms_norm(k.reshape(b, s, N_KV_HEADS, HEAD_DIM), k_norm_g[l])
        v = v.reshape(b, s, N_KV_HEADS, HEAD_DIM)
        q = apply_axial_rope(q, rope)
        k = apply_axial_rope(k, rope)
        attn = gqa_attention(q, k, v)
        attn = rms_norm(attn, attn_out_g[l]).reshape(b, s, ATTN_WIDTH)

        a, g = jnp.split(glu, 2, axis=-1)
        h = a * jax.nn.sigmoid(g)
        h = depthwise_conv(h, conv_dw_w[l], conv_dw_b[l])
        h = layer_norm(h, conv_ln_g[l], conv_ln_b[l])
        h = jax.nn.silu(h) @ w_conv_pw2[l]
        h = rms_norm(h.reshape(b, s, CONV_GROUPS, CONV_GROUP_DIM), conv_out_g[l])
        h = h.reshape(b, s, CONV_CH)

        mixed = jnp.concatenate([attn, h], axis=-1) @ w_o[l]
        x = layer_norm(ALPHA * x + gate1 * mixed, ln1_g[l], ln1_b[l])

        u = adaln_modulate(x, shift2, scale2)
        hf = depthwise_conv(u @ w_up[l], ffn_dw_w[l], ffn_dw_b[l])
        val, gt = jnp.split(hf, 2, axis=-1)
        ffn = (jax.nn.gelu(gt, approximate=False) * val) @ w_down[l]
        x = layer_norm(ALPHA * x + gate2 * ffn, ln2_g[l], ln2_b[l])

    return x
```

```python
import contextlib
import numpy as np
import concourse.bass as bass
import concourse.mybir as mybir
from concourse.bass_utils import run_bass_kernel_spmd

F32 = mybir.dt.float32
BF16 = mybir.dt.bfloat16
ALU = mybir.AluOpType
AF = mybir.ActivationFunctionType

ENGS = ("tensor", "vector", "scalar", "gpsimd", "sync")

D = 1024
SEQ = 8192
NB = 2
NCORE = 8
OWN = 2048
HALF = 1024
HT = 1056
NCH = 3
CW = 352
HPAD = 15
DFF = 2816
NFC = 22
ALPHA = 2.0 ** 0.25
EPS = 1e-6
NKT = SEQ // 128
TPS = 2

V_C = 0; V_BADA = 8; V_QG = 56; V_KG = 58; V_DWB = 60; V_CLNG = 64; V_CLNB = 68; V_COG = 72; V_AOG = 76
V_LN1G = 80; V_LN1B = 88; V_LN2G = 96; V_LN2B = 104; V_FB = 112; V_FW = 156; V_DW = 288; V_MASK = 412; NV = 416
M_SH1 = 0; M_SC1 = 8; M_G1 = 16; M_SH2 = 24; M_SC2 = 32; M_G2 = 40; M_SC1P = 48; M_SC2P = 56


class DmaSlot:
    def __init__(self, sem):
        self.sem = sem
        self.count = 0


class Prog:
    def __init__(self, nc, stack):
        self.nc = nc
        self.stack = stack
        self.sem = {e: stack.enter_context(nc.semaphore("sem_" + e)) for e in ENGS}
        self.cnt = {e: 0 for e in ENGS}
        self.prog = {e: [] for e in ENGS}
        self.waited = {e: {} for e in ENGS}
        self.lastw = {}
        self.rds = {}
        self.all_slots = []

    def slot(self, name=None):
        s = DmaSlot(self.stack.enter_context(self.nc.semaphore(name or f"dsl{len(self.all_slots)}")))
        self.all_slots.append(s)
        return s

    def _need(self, eng, deps):
        best = {}
        for (s, v, src) in deps:
            if src == eng and eng == "tensor":
                continue
            k = id(s)
            if v > self.waited[eng].get(k, 0) and v > best.get(k, (None, 0))[1]:
                best[k] = (s, v)
        for k, (s, v) in best.items():
            self.waited[eng][k] = v
            self.prog[eng].append(lambda e, s=s, v=v: e.wait_ge(s, v))

    def _deps(self, reads, writes):
        deps = []
        for r in reads:
            if r in self.lastw:
                deps.append(self.lastw[r])
        for w in writes:
            if w in self.lastw:
                deps.append(self.lastw[w])
            deps.extend(self.rds.get(w, {}).values())
        return deps

    def _record(self, rec, reads, writes):
        for r in reads:
            self.rds.setdefault(r, {})[id(rec[0])] = rec
        for w in writes:
            self.lastw[w] = rec
            self.rds[w] = {}

    def op(self, eng, fn, reads=(), writes=()):
        self._need(eng, self._deps(reads, writes))
        self.cnt[eng] += 1
        v = self.cnt[eng]
        s = self.sem[eng]
        self.prog[eng].append(lambda e, fn=fn, s=s: fn(e).then_inc(s, 1))
        self._record((s, v, eng), reads, writes)

    def dma(self, q, slot, items, reads=(), writes=()):
        self._need(q, self._deps(reads, writes))
        for (o, i) in items:
            slot.count += 16
            self.prog[q].append(lambda e, o=o, i=i, s=slot.sem: e.dma_start(out=o, in_=i).then_inc(s, 16))
        self._record((slot.sem, slot.count, None), reads, writes)

    def barrier(self):
        deps = [(self.sem[f], self.cnt[f], f) for f in ENGS if self.cnt[f] > 0]
        deps += [(sl.sem, sl.count, None) for sl in self.all_slots if sl.count > 0]
        for e in ENGS:
            self._need(e, [d for d in deps if d[2] != e])

    def emit(self):
        nc = self.nc
        with nc.Block() as block:
            @block.sync
            def _(e):
                for f in self.prog["sync"]:
                    f(e)

            @block.tensor
            def _(e):
                for f in self.prog["tensor"]:
                    f(e)

            @block.vector
            def _(e):
                for f in self.prog["vector"]:
                    f(e)

            @block.scalar
            def _(e):
                for f in self.prog["scalar"]:
                    f(e)

            @block.gpsimd
            def _(e):
                for f in self.prog["gpsimd"]:
                    f(e)


class Arena:
    def __init__(self, ap32, ncols32):
        self.ap = ap32
        self.nbytes = ncols32 * 4
        self.limit = self.nbytes
        self.top = 0

    def alloc(self, free_shape, dtype):
        esz = 4 if dtype == F32 else 2
        n = 1
        for s in free_shape:
            n *= s
        nb = (n * esz + 63) // 64 * 64
        off = self.top
        self.top += nb
        assert self.top <= self.limit, f"arena overflow {self.top} > {self.limit}"
        v = self.ap[:, off // 4:(off + nb) // 4]
        if dtype != F32:
            v = v.bitcast(dtype)
        v = v[:, 0:n]
        if len(free_shape) == 2:
            v = v.rearrange("p (a b) -> p a b", a=free_shape[0])
        elif len(free_shape) == 3:
            v = v.rearrange("p (a b c) -> p a b c", a=free_shape[0], b=free_shape[1])
        return v


def build_program(debug=False):
    nc = bass.Bass("TRN2", target_bir_lowering=False)

    def din(name, shape):
        return nc.dram_tensor(name, list(shape), F32, kind="ExternalInput").ap()

    xT_all = din("xT_all", [D, SEQ])
    xT_own = din("xT_own", [2, D, HT])
    vecs_d = din("vecs", [128, NV])
    w_ada_d = din("w_ada_p", [6, 128, 8, 1024])
    w_in_d = din("w_in_p", [128, 8, 2432])
    ropeq_d = din("ropeq", [2, 2, 128, HT])
    ropek_d = din("ropek", [2, 128, SEQ])
    consts_d = din("consts", [128, 256])
    pw2_d = din("w_pw2_p", [128, 4, 512])
    wo_d = din("w_o_p", [128, 8, 1024])
    wup_d = din("w_up_p", [44, 128, 8, 128])
    wdn_d = din("w_down_p", [8, 128, NFC, 128])
    outT = nc.dram_tensor("outT", [D, OWN], F32, kind="ExternalOutput").ap()
    dbg = None
    if debug:
        dbg = nc.dram_tensor("dbg", [128, 8, HT], F32, kind="ExternalOutput").ap()

    xT_all_v = xT_all.rearrange("(kc p) t -> p kc t", p=128)
    outT_v = outT.rearrange("(kc p) t -> p kc t", p=128)

    NA = 51200
    with contextlib.ExitStack() as st:
        P = Prog(nc, st)
        arena_t = st.enter_context(nc.sbuf_tensor("arena", [128, NA], F32))
        ps = st.enter_context(nc.psum_tensor("ps", [128, 8, 512], F32))
        AR = Arena(arena_t[:, :], NA)

        def pk(b):
            return f"ps{b}"

        op = P.op

        vecs = AR.alloc([NV], F32)
        consts = AR.alloc([256], F32)
        mod = AR.alloc([64], F32)
        ident = AR.alloc([128], BF16)
        ones1024 = AR.alloc([128], BF16)
        ones512 = AR.alloc([128], BF16)
        c_act = AR.alloc([8], BF16)
        kT = AR.alloc([SEQ], BF16)
        Vx = AR.alloc([NKT, 192], BF16)
        blk = consts[:, 128:256]
        xb = [AR.alloc([512], BF16) for _ in range(3)]
        sqb = [AR.alloc([512], BF16) for _ in range(3)]
        s_mean = AR.alloc([512], F32)
        s_var = AR.alloc([512], F32)
        s_rstd = AR.alloc([512], F32)
        tAB = AR.alloc([2048], F32)
        tA = [tAB[:, i * 512:(i + 1) * 512] for i in range(2)]
        tB = [tAB[:, (2 + i) * 512:(3 + i) * 512] for i in range(2)]
        tAB_bf = tAB.bitcast(BF16)
        r_sq = AR.alloc([512], F32)
        r_v = AR.alloc([512], F32)
        r_r = AR.alloc([512], F32)
        r_t1 = AR.alloc([512], F32)
        r_t2 = AR.alloc([512], F32)
        sc_bf = AR.alloc([8], BF16)
        sh_bf = AR.alloc([8], BF16)
        ones_row = AR.alloc([512], BF16)
        mr_row = [AR.alloc([512], BF16) for _ in range(2)]
        base_top = AR.top

        sl_const = P.slot("sl_const")
        P.dma("sync", sl_const, [(vecs, vecs_d), (consts, consts_d)], writes=["vecs", "consts"])
        op("vector", lambda e: e.tensor_copy(out=ident, in_=consts[:, 0:128]), reads=["consts"], writes=["ident"])
        op("vector", lambda e: e.memset(ones1024, 1.0 / 1024.0), writes=["ones1024"])
        op("vector", lambda e: e.memset(ones512, 1.0 / 512.0), writes=["ones512"])
        op("gpsimd", lambda e: e.memset(Vx[:, :, 64:128], 1.0), writes=["Vx"])

        uid = [0]

        def cnt():
            uid[0] += 1
            return uid[0]

        def ln_stats(srcs, T, ones, ones_key, bm, bx):
            n = len(srcs)
            for k, (ap, key) in enumerate(srcs):
                i = cnt() % 3
                if k % 2 == 0:
                    op("vector", lambda e, ap=ap, i=i: e.tensor_copy(out=xb[i][:, :T], in_=ap), reads=[key], writes=[f"xb{i}"])
                else:
                    op("scalar", lambda e, ap=ap, i=i: e.activation(out=xb[i][:, :T], in_=ap, func=AF.Identity), reads=[key], writes=[f"xb{i}"])
                op("scalar", lambda e, ap=ap, i=i: e.activation(out=sqb[i][:, :T], in_=ap, func=AF.Square),
                   reads=[key], writes=[f"sqb{i}"])
                op("tensor", lambda e, i=i, k=k: e.matmul(ps[:, bm, :T], lhsT=ones, rhs=xb[i][:, :T], start=(k == 0), stop=(k == n - 1)),
                   reads=[f"xb{i}", ones_key], writes=[pk(bm)])
                op("tensor", lambda e, i=i, k=k: e.matmul(ps[:, bx, :T], lhsT=ones, rhs=sqb[i][:, :T], start=(k == 0), stop=(k == n - 1)),
                   reads=[f"sqb{i}", ones_key], writes=[pk(bx)])
            op("scalar", lambda e: e.activation(out=s_mean[:, :T], in_=ps[:, bm, :T], func=AF.Identity), reads=[pk(bm)], writes=["s_mean"])
            op("scalar", lambda e: e.activation(out=s_var[:, :T], in_=ps[:, bm, :T], func=AF.Square), reads=[pk(bm)], writes=["s_var"])
            op("vector", lambda e: e.tensor_tensor(out=s_var[:, :T], in0=ps[:, bx, :T], in1=s_var[:, :T], op=ALU.subtract),
               reads=[pk(bx), "s_var"], writes=["s_var"])
            op("vector", lambda e: e.tensor_scalar(out=s_var[:, :T], in0=s_var[:, :T], scalar1=0.0, scalar2=EPS, op0=ALU.max, op1=ALU.add),
               reads=["s_var"], writes=["s_var"])
            op("scalar", lambda e: e.activation(out=s_rstd[:, :T], in_=s_var[:, :T], func=AF.Ln), reads=["s_var"], writes=["s_rstd"])
            op("scalar", lambda e: e.activation(out=s_rstd[:, :T], in_=s_rstd[:, :T], func=AF.Exp, scale=-0.5),
               reads=["s_rstd"], writes=["s_rstd"])

        def ln_stats_E(ap, key, T):
            i = cnt() % 3
            op("vector", lambda e, ap=ap, i=i: e.tensor_copy(out=xb[i][:, :T], in_=ap), reads=[key], writes=[f"xb{i}"])
            op("scalar", lambda e, ap=ap, i=i: e.activation(out=sqb[i][:, :T], in_=ap, func=AF.Square),
               reads=[key], writes=[f"sqb{i}"])
            return i

        def ln_stats_M(i, k, n, T, ones, ones_key, bm, bx):
            op("tensor", lambda e, i=i, k=k: e.matmul(ps[:, bm, :T], lhsT=ones, rhs=xb[i][:, :T], start=(k == 0), stop=(k == n - 1)),
               reads=[f"xb{i}", ones_key], writes=[pk(bm)])
            op("tensor", lambda e, i=i, k=k: e.matmul(ps[:, bx, :T], lhsT=ones, rhs=sqb[i][:, :T], start=(k == 0), stop=(k == n - 1)),
               reads=[f"sqb{i}", ones_key], writes=[pk(bx)])

        def ln_stats_fin(T, bm, bx):
            op("scalar", lambda e: e.activation(out=s_mean[:, :T], in_=ps[:, bm, :T], func=AF.Identity), reads=[pk(bm)], writes=["s_mean"])
            op("scalar", lambda e: e.activation(out=s_var[:, :T], in_=ps[:, bm, :T], func=AF.Square), reads=[pk(bm)], writes=["s_var"])
            op("vector", lambda e: e.tensor_tensor(out=s_var[:, :T], in0=ps[:, bx, :T], in1=s_var[:, :T], op=ALU.subtract),
               reads=[pk(bx), "s_var"], writes=["s_var"])
            op("vector", lambda e: e.tensor_scalar(out=s_var[:, :T], in0=s_var[:, :T], scalar1=0.0, scalar2=EPS, op0=ALU.max, op1=ALU.add),
               reads=["s_var"], writes=["s_var"])
            op("scalar", lambda e: e.activation(out=s_rstd[:, :T], in_=s_var[:, :T], func=AF.Ln), reads=["s_var"], writes=["s_rstd"])
            op("scalar", lambda e: e.activation(out=s_rstd[:, :T], in_=s_rstd[:, :T], func=AF.Exp, scale=-0.5),
               reads=["s_rstd"], writes=["s_rstd"])

        def ln_apply(srcs, dsts, T, scales, biases, sb_keys):
            for k, ((ap, key), (dap, dkey)) in enumerate(zip(srcs, dsts)):
                i = cnt() % 2
                op("vector", lambda e, ap=ap, i=i: e.tensor_tensor(out=tA[i][:, :T], in0=ap, in1=s_mean[:, :T], op=ALU.subtract),
                   reads=[key, "s_mean"], writes=[f"tA{i}"])
                op("vector", lambda e, i=i: e.tensor_tensor(out=tB[i][:, :T], in0=tA[i][:, :T], in1=s_rstd[:, :T], op=ALU.mult),
                   reads=[f"tA{i}", "s_rstd"], writes=[f"tB{i}"])
                op("scalar", lambda e, dap=dap, i=i, k=k: e.activation(out=dap, in_=tB[i][:, :T], func=AF.Identity,
                                                                      scale=scales[k], bias=biases[k]),
                   reads=[f"tB{i}"] + list(sb_keys), writes=[dkey])

        def ln_apply_fold(srcs, dsts, T, mrb):
            for k, ((ap, key), (dap, dkey)) in enumerate(zip(srcs, dsts)):
                op("vector", lambda e, ap=ap, dap=dap, k=k: e.scalar_tensor_tensor(
                    out=dap, in0=ap, scalar=mcol(M_SC1P + k), in1=s_rstd[:, :T], op0=ALU.mult, op1=ALU.mult),
                   reads=[key, "mod", "s_rstd"], writes=[dkey])
            op("vector", lambda e, mrb=mrb: e.tensor_tensor(out=mr_row[mrb][0:1, :T], in0=s_mean[0:1, :T], in1=s_rstd[0:1, :T], op=ALU.mult),
               reads=["s_mean", "s_rstd"], writes=[f"mr{mrb}"])

        def fold_rows(W, wkey, N, ncs_row, sb_row, rkey, bank):
            for n0 in range(0, N, 512):
                w = min(512, N - n0)
                for (col, row, sgn) in ((sc_bf, ncs_row, -1.0), (sh_bf, sb_row, 1.0)):
                    for kc in range(8):
                        op("tensor", lambda e, col=col, kc=kc, n0=n0, w=w: e.matmul(
                            ps[0:1, bank, :w], lhsT=col[:, kc:kc + 1], rhs=W[:, kc, n0:n0 + w], start=(kc == 0), stop=(kc == 7)),
                           reads=[wkey, "scsh"], writes=[pk(bank)])
                    op("scalar", lambda e, row=row, n0=n0, w=w, sgn=sgn: e.activation(
                        out=row[0:1, n0:n0 + w], in_=ps[0:1, bank, :w], func=AF.Identity, scale=sgn),
                       reads=[pk(bank)], writes=[rkey])

        def fold_mm(bank, T, ncs_row, sb_row, rkey, c0, mrb):
            op("tensor", lambda e: e.matmul(ps[:, bank, :T], lhsT=ncs_row[0:1, c0:c0 + 128], rhs=mr_row[mrb][0:1, :T], start=False, stop=False),
               reads=[rkey, f"mr{mrb}"], writes=[pk(bank)])
            op("tensor", lambda e: e.matmul(ps[:, bank, :T], lhsT=sb_row[0:1, c0:c0 + 128], rhs=ones_row[0:1, :T], start=False, stop=True),
               reads=[rkey, "ones_row"], writes=[pk(bank)])

        def rsqrt_from_psum(bank, T):
            op("vector", lambda e: e.tensor_scalar_add(out=r_v[:, :T], in0=ps[:, bank, :T], scalar1=EPS), reads=[pk(bank)], writes=["r_v"])
            op("scalar", lambda e: e.activation(out=r_v[:, :T], in_=r_v[:, :T], func=AF.Ln), reads=["r_v"], writes=["r_v"])
            op("scalar", lambda e: e.activation(out=r_r[:, :T], in_=r_v[:, :T], func=AF.Exp, scale=-0.5), reads=["r_v"], writes=["r_r"])

        def rope_rms(bA, bB, bR, gcol, C, S, ckeys, out, okey, T):
            op("scalar", lambda e: e.activation(out=r_sq[:, :T], in_=ps[:, bA, :T], func=AF.Square), reads=[pk(bA)], writes=["r_sq"])
            op("tensor", lambda e: e.matmul(ps[:, bR, :T], lhsT=blk, rhs=r_sq[:, :T], start=True, stop=True),
               reads=["r_sq", "consts"], writes=[pk(bR)])
            rsqrt_from_psum(bR, T)
            op("vector", lambda e: e.scalar_tensor_tensor(out=r_t1[:, :T], in0=ps[:, bA, :T], scalar=vecs[:, gcol:gcol + 1], in1=C,
                                                          op0=ALU.mult, op1=ALU.mult),
               reads=[pk(bA), "vecs"] + list(ckeys), writes=["r_t1"])
            op("vector", lambda e: e.scalar_tensor_tensor(out=r_t2[:, :T], in0=ps[:, bB, :T], scalar=vecs[:, gcol + 1:gcol + 2], in1=S,
                                                          op0=ALU.mult, op1=ALU.mult),
               reads=[pk(bB), "vecs"] + list(ckeys), writes=["r_t2"])
            op("vector", lambda e: e.tensor_tensor(out=r_t1[:, :T], in0=r_t1[:, :T], in1=r_t2[:, :T], op=ALU.add),
               reads=["r_t1", "r_t2"], writes=["r_t1"])
            if isinstance(out, list):
                for (p0, p1, oap) in out:
                    op("vector", lambda e, p0=p0, p1=p1, oap=oap: e.tensor_tensor(out=oap, in0=r_t1[p0:p1, :T], in1=r_r[p0:p1, :T], op=ALU.mult),
                       reads=["r_t1", "r_r"], writes=[okey])
            else:
                op("vector", lambda e: e.tensor_tensor(out=out, in0=r_t1[:, :T], in1=r_r[:, :T], op=ALU.mult),
                   reads=["r_t1", "r_r"], writes=[okey])

        def sigmoid_recip(src_ap, src_key, T, dst, dkey):
            op("scalar", lambda e: e.activation(out=dst, in_=src_ap, func=AF.Exp, scale=-1.0), reads=[src_key], writes=[dkey])
            op("vector", lambda e: e.tensor_scalar_add(out=dst, in0=dst, scalar1=1.0), reads=[dkey], writes=[dkey])
            op("vector", lambda e: e.reciprocal(out=dst, in_=dst), reads=[dkey], writes=[dkey])

        mark = AR.top
        wa = [AR.alloc([8, 1024], BF16) for _ in range(2)]
        sl_wa = [P.slot(f"sl_wa{i}") for i in range(2)]
        c_tmp = AR.alloc([8], F32)
        sigmoid_recip(vecs[:, V_C:V_C + 8], "vecs", 8, c_tmp, "c_tmp")
        op("vector", lambda e: e.tensor_tensor(out=c_act, in0=vecs[:, V_C:V_C + 8], in1=c_tmp, op=ALU.mult),
           reads=["vecs", "c_tmp"], writes=["c_act"])
        def mod_dma(part):
            b = part % 2
            P.dma("gpsimd", sl_wa[b], [(wa[b], w_ada_d[part])], writes=[f"wa{b}"])

        def mod_mm(part, bank):
            b = part % 2
            for nn in range(8):
                n = part * 8 + nn
                for kc in range(8):
                    op("tensor", lambda e, b=b, nn=nn, n=n, kc=kc: e.matmul(
                        ps[:, bank, n:n + 1], lhsT=wa[b][:, kc, nn * 128:(nn + 1) * 128], rhs=c_act[:, kc:kc + 1],
                        start=(kc == 0), stop=(kc == 7)),
                       reads=[f"wa{b}", "c_act"], writes=[pk(bank)])

        def mod_fin_rest():
            op("vector", lambda e: e.tensor_tensor(out=mod[:, 16:48], in0=ps[:, 7, 16:48], in1=vecs[:, V_BADA + 16:V_BADA + 48], op=ALU.add),
               reads=[pk(7), "vecs"], writes=["mod"])
            op("vector", lambda e: e.tensor_scalar_add(out=mod[:, M_SC2P:M_SC2P + 8], in0=mod[:, M_SC2:M_SC2 + 8], scalar1=1.0),
               reads=["mod"], writes=["mod"])

        mod_dma(0)
        mod_dma(1)
        mod_mm(0, 0)
        mod_mm(1, 0)
        op("vector", lambda e: e.tensor_tensor(out=mod[:, 0:16], in0=ps[:, 0, 0:16], in1=vecs[:, V_BADA:V_BADA + 16], op=ALU.add),
           reads=[pk(0), "vecs"], writes=["mod"])
        op("vector", lambda e: e.tensor_scalar_add(out=mod[:, M_SC1P:M_SC1P + 8], in0=mod[:, M_SC1:M_SC1 + 8], scalar1=1.0),
           reads=["mod"], writes=["mod"])
        op("vector", lambda e: e.tensor_copy(out=sc_bf, in_=mod[:, M_SC1P:M_SC1P + 8]), reads=["mod"], writes=["scsh"])
        op("vector", lambda e: e.tensor_copy(out=sh_bf, in_=mod[:, M_SH1:M_SH1 + 8]), reads=["mod"], writes=["scsh"])
        op("vector", lambda e: e.memset(ones_row, 1.0), writes=["ones_row"])
        P.barrier()
        mark0 = mark

        def mcol(c):
            return mod[:, c:c + 1]

        Wkv = AR.alloc([8, 384], BF16)
        xc = [AR.alloc([8, 512], F32) for _ in range(2)]
        rkC = [AR.alloc([512], F32) for _ in range(2)]
        rkS = [AR.alloc([512], F32) for _ in range(2)]
        uT = [AR.alloc([8, 512], BF16) for _ in range(2)]
        sl_wkv = P.slot("sl_wkv")
        sl_xc = [P.slot(f"sl_xc{i}") for i in range(2)]
        sl_rk = [P.slot(f"sl_rk{i}") for i in range(2)]
        P.dma("gpsimd", sl_wkv, [(Wkv, w_in_d[:, :, 2048:2432])], writes=["Wkv"])
        mod_dma(2)
        mod_dma(3)
        ncs_k = AR.alloc([384], BF16)
        sb_k = AR.alloc([384], BF16)
        def kv_A(ck):
            b = ck % 2
            cs = slice(ck * 512, (ck + 1) * 512)
            P.dma("sync", sl_xc[b], [(xc[b], xT_all_v[:, :, cs])], writes=[f"xc{b}"])
            P.dma("sync", sl_rk[b], [(rkC[b], ropek_d[0][:, cs]), (rkS[b], ropek_d[1][:, cs])], writes=[f"rk{b}"])
            srcs = [(xc[b][:, k, :], f"xc{b}") for k in range(8)]
            ln_stats(srcs, 512, ones1024, "ones1024", 0, 1)
            ln_apply_fold(srcs, [(uT[b][:, k, :], f"uT{b}") for k in range(8)], 512, b)

        def kv_B(ck):
            b = ck % 2
            cs = slice(ck * 512, (ck + 1) * 512)
            for (bank, c0) in ((2, 0), (3, 128)):
                for kc in range(8):
                    op("tensor", lambda e, bank=bank, c0=c0, kc=kc, b=b: e.matmul(
                        ps[:, bank, :], lhsT=Wkv[:, kc, c0:c0 + 128], rhs=uT[b][:, kc, :], start=(kc == 0), stop=False),
                       reads=["Wkv", f"uT{b}"], writes=[pk(bank)])
                fold_mm(bank, 512, ncs_k, sb_k, "rows_k", c0, b)
            for s4 in range(4):
                for kc in range(8):
                    op("tensor", lambda e, s4=s4, kc=kc, b=b: e.matmul(
                        ps[:, 4, s4 * 128:(s4 + 1) * 128], lhsT=uT[b][:, kc, s4 * 128:(s4 + 1) * 128], rhs=Wkv[:, kc, 256:384],
                        start=(kc == 0), stop=False),
                       reads=["Wkv", f"uT{b}"], writes=[pk(4)])
                op("tensor", lambda e, s4=s4, b=b: e.matmul(
                    ps[:, 4, s4 * 128:(s4 + 1) * 128], lhsT=mr_row[b][0:1, s4 * 128:(s4 + 1) * 128], rhs=ncs_k[0:1, 256:384], start=False, stop=False),
                   reads=["rows_k", f"mr{b}"], writes=[pk(4)])
                op("tensor", lambda e, s4=s4: e.matmul(
                    ps[:, 4, s4 * 128:(s4 + 1) * 128], lhsT=ones_row[0:1, 0:128], rhs=sb_k[0:1, 256:384], start=False, stop=True),
                   reads=["rows_k", "ones_row"], writes=[pk(4)])
            psv = ps[:, 4, :].rearrange("p (a b) -> p a b", a=4)
            op("scalar", lambda e, ck=ck: e.activation(out=Vx[:, ck * 4:(ck + 1) * 4, 0:64], in_=psv[:, :, 0:64], func=AF.Identity),
               reads=[pk(4)], writes=["Vx"])
            op("scalar", lambda e, ck=ck: e.activation(out=Vx[:, ck * 4:(ck + 1) * 4, 128:192], in_=psv[:, :, 64:128], func=AF.Identity),
               reads=[pk(4)], writes=["Vx"])
            rope_rms(2, 3, 5, V_KG, rkC[b], rkS[b], [f"rk{b}"], kT[:, cs], "kT", 512)

        NKC = SEQ // 512
        kv_A(0)
        fold_rows(Wkv, "Wkv", 384, ncs_k, sb_k, "rows_k", 6)
        for ck in range(NKC):
            if ck + 1 < NKC:
                kv_A(ck + 1)
            kv_B(ck)
            if ck == 4:
                mod_mm(2, 7)
                mod_dma(4)
            elif ck == 5:
                mod_mm(3, 7)
                mod_dma(5)
            elif ck == 9:
                mod_mm(4, 7)
            elif ck == 10:
                mod_mm(5, 7)
                mod_fin_rest()
        P.barrier()
        AR.top = mark0

        sl_out = [P.slot(f"sl_out{i}") for i in range(3)]
        sl_wown = P.slot("sl_wown")
        sl_rq = P.slot("sl_rq")
        sl_xo = [P.slot(f"sl_xo{i}") for i in range(2)]
        sl_pw2 = P.slot("sl_pw2")
        sl_wo = P.slot("sl_wo")
        sl_wd = [P.slot(f"sl_wd{i}") for i in range(3)]
        sl_dbg = P.slot("sl_dbg") if debug else None

        X1_BYTES = 8 * HT * 4
        X1 = arena_t[:, NA - 8 * HT:NA].rearrange("p (a b) -> p a b", a=8)
        sl_wu = [P.slot(f"sl_wu{i}") for i in range(4)]
        for h in range(2):
            hmark = AR.top
            AR.limit = AR.nbytes
            qTp = AR.alloc([8, HT], BF16)
            AT = AR.alloc([8, HT], BF16)
            hp_mark = AR.top
            Hp = AR.alloc([4, HT + 2 * HPAD], BF16)
            xT_own_v = xT_own[h].rearrange("(kc p) t -> p kc t", p=128)
            mL = vecs[:, V_MASK + 2 * h:V_MASK + 2 * h + 1]
            mR = vecs[:, V_MASK + 2 * h + 1:V_MASK + 2 * h + 2]

            mark = AR.top
            Wown = AR.alloc([8, 2048], BF16)
            rqC = AR.alloc([HT], F32)
            rqS = AR.alloc([HT], F32)
            xo = [AR.alloc([8, CW], F32) for _ in range(2)]
            uo = [AR.alloc([8, CW], BF16) for _ in range(2)]
            sg = [AR.alloc([CW], F32) for _ in range(2)]
            P.dma("gpsimd", sl_wown, [(Wown, w_in_d[:, :, 0:2048])], writes=["Wown"])
            ncs_o = tAB_bf[:, 0:2048]
            sb_o = tAB_bf[:, 2048:4096]
            P.dma("sync", sl_rq, [(rqC, ropeq_d[h, 0]), (rqS, ropeq_d[h, 1])], writes=["rq"])
            op("gpsimd", lambda e: e.memset(Hp[:, :, 0:HPAD], 0.0), writes=["Hp"])
            op("gpsimd", lambda e: e.memset(Hp[:, :, HPAD + HT:HT + 2 * HPAD], 0.0), writes=["Hp"])
            op("gpsimd", lambda e: e.memset(qTp, 0.0), writes=["qT"])

            def own_A(c, xT_own_v=xT_own_v):
                b = c % 2
                cs = slice(c * CW, (c + 1) * CW)
                P.dma("sync", sl_xo[b], [(xo[b], xT_own_v[:, :, cs])], writes=[f"xo{b}"])
                srcs = [(xo[b][:, k, :], f"xo{b}") for k in range(8)]
                ln_stats(srcs, CW, ones1024, "ones1024", 0, 1)
                ln_apply_fold(srcs, [(uo[b][:, k, :], f"uo{b}") for k in range(8)], CW, b)

            def own_B(c):
                b = c % 2
                cs = slice(c * CW, (c + 1) * CW)
                for j in range(4):
                    bq = 2 + 2 * (j % 2)
                    for (bank, c0) in ((bq, j * 128), (bq + 1, 512 + j * 128)):
                        for kc in range(8):
                            op("tensor", lambda e, bank=bank, c0=c0, kc=kc, b=b: e.matmul(
                                ps[:, bank, :CW], lhsT=Wown[:, kc, c0:c0 + 128], rhs=uo[b][:, kc, :], start=(kc == 0), stop=False),
                               reads=["Wown", f"uo{b}"], writes=[pk(bank)])
                        fold_mm(bank, CW, ncs_o, sb_o, "rows_o", c0, b)
                    rope_rms(bq, bq + 1, 6, V_QG, rqC[:, cs], rqS[:, cs], ["rq"],
                             [(0, 64, qTp[0:64, 2 * j, cs]), (64, 128, qTp[64:128, 2 * j + 1, cs])], "qT", CW)
                for i in range(4):
                    bq = 2 + 2 * (i % 2)
                    for (bank, c0) in ((bq, 1024 + i * 128), (bq + 1, 1536 + i * 128)):
                        for kc in range(8):
                            op("tensor", lambda e, bank=bank, c0=c0, kc=kc, b=b: e.matmul(
                                ps[:, bank, :CW], lhsT=Wown[:, kc, c0:c0 + 128], rhs=uo[b][:, kc, :], start=(kc == 0), stop=False),
                               reads=["Wown", f"uo{b}"], writes=[pk(bank)])
                        fold_mm(bank, CW, ncs_o, sb_o, "rows_o", c0, b)
                    si = i % 2
                    sigmoid_recip(ps[:, bq + 1, :CW], pk(bq + 1), CW, sg[si], f"sg{si}")
                    op("vector", lambda e, bq=bq, i=i, si=si, c=c: e.tensor_tensor(
                        out=Hp[:, i, HPAD + c * CW:HPAD + (c + 1) * CW], in0=ps[:, bq, :CW], in1=sg[si], op=ALU.mult),
                       reads=[pk(bq), f"sg{si}"], writes=["Hp"])

            own_A(0)
            own_A(1)
            fold_rows(Wown, "Wown", 2048, ncs_o, sb_o, "rows_o", 7)
            for c in range(NCH):
                own_B(c)
                if c + 2 < NCH:
                    own_A(c + 2)
            op("vector", lambda e, mL=mL: e.tensor_scalar_mul(out=Hp[:, :, HPAD:HPAD + 16], in0=Hp[:, :, HPAD:HPAD + 16], scalar1=mL),
               reads=["Hp", "vecs"], writes=["Hp"])
            op("vector", lambda e, mR=mR: e.tensor_scalar_mul(out=Hp[:, :, HPAD + HT - 16:HPAD + HT], in0=Hp[:, :, HPAD + HT - 16:HPAD + HT], scalar1=mR),
               reads=["Hp", "vecs"], writes=["Hp"])
            P.barrier()
            AR.top = mark

            mark = AR.top
            dg = AR.alloc([4, 31, 128], BF16)
            pw2 = AR.alloc([4, 512], BF16)
            hcv = [AR.alloc([4, CW], F32) for _ in range(2)]
            zf = [AR.alloc([CW], F32) for _ in range(2)]
            ze = [AR.alloc([CW], F32) for _ in range(2)]
            zT = AR.alloc([4, CW], BF16)
            P.dma("gpsimd", sl_pw2, [(pw2, pw2_d)], writes=["pw2"])
            for i in range(4):
                for j in range(31):
                    col = V_DW + i * 31 + j
                    if j % 2 == 0:
                        op("vector", lambda e, i=i, j=j, col=col: e.tensor_scalar_mul(out=dg[:, i, j, :], in0=ident, scalar1=vecs[:, col:col + 1]),
                           reads=["ident", "vecs"], writes=[f"dg{i}_{j}"])
                    else:
                        op("scalar", lambda e, i=i, j=j, col=col: e.activation(out=dg[:, i, j, :], in_=ident, func=AF.Identity,
                                                                               scale=vecs[:, col:col + 1]),
                           reads=["ident", "vecs"], writes=[f"dg{i}_{j}"])

            def conv_A(c):
                hb = c % 2
                for i in range(4):
                    bank = 2 + (i % 2)
                    for j in range(31):
                        op("tensor", lambda e, i=i, j=j, bank=bank, c=c: e.matmul(
                            ps[:, bank, :CW], lhsT=dg[:, i, j, :], rhs=Hp[:, i, c * CW + j:c * CW + j + CW], start=(j == 0), stop=(j == 30)),
                           reads=[f"dg{i}_{j}", "Hp"], writes=[pk(bank)])
                    op("scalar", lambda e, i=i, bank=bank, hb=hb: e.activation(out=hcv[hb][:, i, :], in_=ps[:, bank, :CW], func=AF.Identity,
                                                                              bias=vecs[:, V_DWB + i:V_DWB + i + 1]),
                       reads=[pk(bank), "vecs"], writes=[f"hcv{hb}_{i}"])

            def conv_B(c):
                hb = c % 2
                srcs = [(hcv[hb][:, i, :], f"hcv{hb}_{i}") for i in range(4)]
                ln_stats(srcs, CW, ones512, "ones512", 0, 1)
                for i in range(4):
                    zi = i % 2
                    ln_apply([srcs[i]], [(zf[zi], f"zf{zi}")], CW, [vecs[:, V_CLNG + i:V_CLNG + i + 1]], [vecs[:, V_CLNB + i:V_CLNB + i + 1]], ["vecs"])
                    sigmoid_recip(zf[zi], f"zf{zi}", CW, ze[zi], f"ze{zi}")
                    op("vector", lambda e, i=i, zi=zi: e.tensor_tensor(out=zT[:, i, :], in0=zf[zi], in1=ze[zi], op=ALU.mult),
                       reads=[f"zf{zi}", f"ze{zi}"], writes=[f"zT{i}"])
                for o in range(4):
                    bank = 4 + (o % 2)
                    for i in range(4):
                        op("tensor", lambda e, o=o, i=i, bank=bank: e.matmul(
                            ps[:, bank, :CW], lhsT=pw2[:, i, o * 128:(o + 1) * 128], rhs=zT[:, i, :], start=(i == 0), stop=(i == 3)),
                           reads=["pw2", f"zT{i}"], writes=[pk(bank)])
                    op("scalar", lambda e, bank=bank: e.activation(out=r_sq[:, :CW], in_=ps[:, bank, :CW], func=AF.Square),
                       reads=[pk(bank)], writes=["r_sq"])
                    op("tensor", lambda e: e.matmul(ps[:, 6, :CW], lhsT=blk, rhs=r_sq[:, :CW], start=True, stop=True),
                       reads=["r_sq", "consts"], writes=[pk(6)])
                    rsqrt_from_psum(6, CW)
                    op("vector", lambda e, o=o, bank=bank, c=c: e.scalar_tensor_tensor(
                        out=AT[:, 4 + o, c * CW:(c + 1) * CW], in0=ps[:, bank, :CW], scalar=vecs[:, V_COG + o:V_COG + o + 1], in1=r_r[:, :CW],
                        op0=ALU.mult, op1=ALU.mult),
                       reads=[pk(bank), "vecs", "r_r"], writes=["AT"])

            conv_A(0)
            for c in range(NCH):
                if c + 1 < NCH:
                    conv_A(c + 1)
                conv_B(c)
            P.barrier()
            AR.top = hp_mark

            AR.limit = AR.nbytes - X1_BYTES
            mark = AR.top
            CA = 342
            NSB = 3 if TPS == 2 else 2
            LA = NSB - 1
            wo = AR.alloc([8, 1024], BF16)
            pT = [AR.alloc([TPS, CW], BF16) for _ in range(NSB)]
            rec = AR.alloc([CW], F32)
            On = AR.alloc([CW], F32)
            xr = AR.alloc([8, CW], F32)
            yb = AR.alloc([8, CW], F32)
            gt = [AR.alloc([CW], F32) for _ in range(2)]
            P.dma("gpsimd", sl_wo, [(wo, wo_d)], writes=["wo"])
            groups = [list(range(t0, min(t0 + TPS, NKT))) for t0 in range(0, NKT, TPS)]
            carry = []
            for c in range(NCH):
                cs = slice(15 + c * CA, 15 + (c + 1) * CA)
                P.dma("sync", sl_xo[0], [(xr[:, :, :CA], xT_own_v[:, :, cs])], writes=["xr"])
                op("vector", lambda e: e.tensor_scalar_mul(out=xr[:, :, :CA], in0=xr[:, :, :CA], scalar1=ALPHA), reads=["xr"], writes=["xr"])
                items = [(j, hd, gi) for j in range(4) for hd in range(2) for gi in range(len(groups))]

                def qk(idx, c=c, cs=cs, items=items):
                    j, hd, gi = items[idx]
                    sb = idx % NSB
                    for i2, kt in enumerate(groups[gi]):
                        bank = 2 + TPS * sb + i2
                        op("tensor", lambda e, j=j, hd=hd, kt=kt, bank=bank: e.matmul(
                            ps[:, bank, :CA], lhsT=kT[:, kt * 128:(kt + 1) * 128], rhs=qTp[:, 2 * j + hd, cs],
                            start=True, stop=True),
                           reads=["kT", "qT"], writes=[pk(bank)])

                for i in range(LA):
                    qk(i)
                deferred = []
                for n_c, fn_c in enumerate(carry):
                    deferred.append((3 + 3 * n_c, fn_c))
                carry = []
                for idx in range(len(items)):
                    j, hd, gi = items[idx]
                    tiles = groups[gi]
                    nt = len(tiles)
                    sb = idx % NSB
                    b0 = 2 + TPS * sb
                    op("scalar", lambda e, sb=sb, b0=b0, nt=nt: e.activation(out=pT[sb][:, 0:nt, :CA], in_=ps[:, b0:b0 + nt, :CA],
                                                                             func=AF.Exp, scale=0.125),
                       reads=[pk(b0 + i2) for i2 in range(nt)], writes=[f"pT{sb}"])
                    while deferred and deferred[0][0] <= idx:
                        deferred.pop(0)[1]()
                    if idx + LA < len(items):
                        qk(idx + LA)
                    for i2, kt in enumerate(tiles):
                        op("tensor", lambda e, hd=hd, kt=kt, sb=sb, i2=i2: e.matmul(
                            ps[:, hd, :CA], lhsT=Vx[:, kt, hd * 64:hd * 64 + 128], rhs=pT[sb][:, i2, :CA], start=(kt == 0), stop=(kt == NKT - 1)),
                           reads=["Vx", f"pT{sb}"], writes=[pk(hd)])
                    if hd == 1 and gi == len(groups) - 1:
                        op("vector", lambda e: e.reciprocal(out=rec[64:128, :CA], in_=ps[64:128, 0, :CA]), reads=[pk(0)], writes=["rec"])
                        op("vector", lambda e: e.reciprocal(out=rec[0:64, :CA], in_=ps[0:64, 1, :CA]), reads=[pk(1)], writes=["rec"])
                        op("vector", lambda e: e.tensor_tensor(out=On[0:64, :CA], in0=ps[0:64, 0, :CA], in1=rec[64:128, :CA], op=ALU.mult),
                           reads=[pk(0), "rec"], writes=["On"])
                        op("vector", lambda e: e.tensor_tensor(out=On[64:128, :CA], in0=ps[64:128, 1, :CA], in1=rec[0:64, :CA], op=ALU.mult),
                           reads=[pk(1), "rec"], writes=["On"])
                        op("vector", lambda e: e.tensor_tensor(out=r_sq[:, :CA], in0=On[:, :CA], in1=On[:, :CA], op=ALU.mult),
                           reads=["On"], writes=["r_sq"])
                        fb = b0
                        op("tensor", lambda e, fb=fb: e.matmul(ps[:, fb, :CA], lhsT=blk, rhs=r_sq[:, :CA], start=True, stop=True),
                           reads=["r_sq", "consts"], writes=[pk(fb)])
                        op("vector", lambda e, fb=fb: e.tensor_scalar_add(out=r_v[:, :CA], in0=ps[:, fb, :CA], scalar1=EPS),
                           reads=[pk(fb)], writes=["r_v"])

                        def norm_tail(j=j, cs=cs):
                            op("scalar", lambda e: e.activation(out=r_v[:, :CA], in_=r_v[:, :CA], func=AF.Ln), reads=["r_v"], writes=["r_v"])
                            op("scalar", lambda e: e.activation(out=r_r[:, :CA], in_=r_v[:, :CA], func=AF.Exp, scale=-0.5),
                               reads=["r_v"], writes=["r_r"])
                            op("vector", lambda e, j=j, cs=cs: e.scalar_tensor_tensor(
                                out=AT[:, j, cs], in0=On[:, :CA], scalar=vecs[:, V_AOG + j:V_AOG + j + 1], in1=r_r[:, :CA],
                                op0=ALU.mult, op1=ALU.mult),
                               reads=["On", "vecs", "r_r"], writes=["AT"])
                        deferred.append((idx + 2, norm_tail))
                while deferred:
                    deferred.pop(0)[1]()
                for f in range(8):
                    bank = 2 + (f % 2)
                    for k in range(8):
                        op("tensor", lambda e, f=f, k=k, bank=bank, cs=cs: e.matmul(
                            ps[:, bank, :CA], lhsT=wo[:, k, f * 128:(f + 1) * 128], rhs=AT[:, k, cs], start=(k == 0), stop=(k == 7)),
                           reads=["wo", "AT"], writes=[pk(bank)])
                    op("vector", lambda e, f=f, bank=bank: e.scalar_tensor_tensor(out=yb[:, f, :CA], in0=ps[:, bank, :CA], scalar=mcol(M_G1 + f),
                                                                                  in1=xr[:, f, :CA], op0=ALU.mult, op1=ALU.add),
                       reads=[pk(bank), "mod", "xr"], writes=[f"yb{f}"])
                srcs = [(yb[:, f, :CA], f"yb{f}") for f in range(8)]
                ln_stats(srcs, CA, ones1024, "ones1024", 4, 5)

                def ln1_apply_f(f, cs=cs):
                    i = cnt() % 2
                    op("vector", lambda e, f=f, i=i: e.tensor_tensor(out=tA[i][:, :CA], in0=yb[:, f, :CA], in1=s_mean[:, :CA], op=ALU.subtract),
                       reads=[f"yb{f}", "s_mean"], writes=[f"tA{i}"])
                    op("vector", lambda e, i=i: e.tensor_tensor(out=tB[i][:, :CA], in0=tA[i][:, :CA], in1=s_rstd[:, :CA], op=ALU.mult),
                       reads=[f"tA{i}", "s_rstd"], writes=[f"tB{i}"])
                    op("vector", lambda e, f=f, i=i, cs=cs: e.tensor_scalar(
                        out=X1[:, f, cs], in0=tB[i][:, :CA], scalar1=vecs[:, V_LN1G + f:V_LN1G + f + 1], scalar2=vecs[:, V_LN1B + f:V_LN1B + f + 1],
                        op0=ALU.mult, op1=ALU.add),
                       reads=[f"tB{i}", "vecs"], writes=["X1"])

                if c + 1 < NCH:
                    carry = [(lambda f=f, fn=ln1_apply_f: fn(f)) for f in range(8)]
                else:
                    for f in range(8):
                        ln1_apply_f(f)
            if debug and h == 0:
                P.dma("sync", sl_dbg, [(dbg, X1)], reads=["X1"])
            P.barrier()

            AR.top = hmark
            mark = AR.top
            u2 = AR.alloc([8, 514], BF16)
            GT = AR.alloc([NFC, 512], BF16)
            y2 = AR.alloc([8, 512], F32)
            Pb = [AR.alloc([514], BF16) for _ in range(2)]
            Wu = [AR.alloc([8, 128], BF16) for _ in range(4)]
            Wd = [AR.alloc([NFC, 128], BF16) for _ in range(3)]
            d3 = [AR.alloc([3, 128], BF16) for _ in range(2)]
            gl = [r_t1, r_t2]
            ob = [AR.alloc([512], F32) for _ in range(3)]
            g2t = [r_sq, r_v]
            units = [(cc, which) for cc in range(NFC) for which in range(2)]
            NU = len(units)
            pending_ln2 = []
            for gq in range(2):
                w0 = 15 + gq * 512

                def wload(u):
                    cc, which = units[u]
                    wb = u % 4
                    P.dma("gpsimd", sl_wu[wb], [(Wu[wb], wup_d[cc + NFC * which])], writes=[f"Wu{wb}"])

                def dload(f):
                    db = f % 3
                    P.dma("gpsimd", sl_wd[db], [(Wd[db], wdn_d[f])], writes=[f"Wd{db}"])

                def group_head(g):
                    wload(0)
                    wload(1)
                    wg0 = 15 + g * 512
                    for wc in range(2):
                        ws = slice(wg0 + wc * 257, wg0 + (wc + 1) * 257)
                        srcs_u = [(X1[:, k, ws], "X1") for k in range(8)]
                        ln_stats(srcs_u, 257, ones1024, "ones1024", 0, 1)
                        ln_apply(srcs_u, [(u2[:, k, wc * 257:(wc + 1) * 257], "u2") for k in range(8)], 257,
                                 [mcol(M_SC2P + k) for k in range(8)], [mcol(M_SH2 + k) for k in range(8)], ["mod"])

                if gq == 0:
                    group_head(0)

                def ffn_P(u, gq=gq, mL=mL, mR=mR):
                    cc, which = units[u]
                    chunk = cc + NFC * which
                    pb = u % 2
                    wb = u % 4
                    if u + 2 < NU:
                        wload(u + 2)
                    for j in range(3):
                        col = V_FW + j * 44 + chunk
                        op("vector", lambda e, pb=pb, j=j, col=col: e.tensor_scalar_mul(
                            out=d3[pb][:, j, :], in0=ident, scalar1=vecs[:, col:col + 1]),
                           reads=["ident", "vecs"], writes=[f"d3_{pb}"])
                    for wc in range(2):
                        bank = 2 + 2 * pb + wc
                        for kc in range(8):
                            op("tensor", lambda e, wb=wb, kc=kc, wc=wc, bank=bank: e.matmul(
                                ps[:, bank, :257], lhsT=Wu[wb][:, kc, :], rhs=u2[:, kc, wc * 257:(wc + 1) * 257], start=(kc == 0), stop=(kc == 7)),
                               reads=[f"Wu{wb}", "u2"], writes=[pk(bank)])
                        if wc == 0:
                            op("scalar", lambda e, pb=pb, bank=bank: e.activation(out=Pb[pb][:, 0:257], in_=ps[:, bank, :257], func=AF.Identity),
                               reads=[pk(bank)], writes=[f"Pb{pb}"])
                        else:
                            op("vector", lambda e, pb=pb, bank=bank: e.tensor_copy(out=Pb[pb][:, 257:514], in_=ps[:, bank, :257]),
                               reads=[pk(bank)], writes=[f"Pb{pb}"])
                    if gq == 0:
                        op("vector", lambda e, pb=pb, mL=mL: e.tensor_scalar_mul(out=Pb[pb][:, 0:1], in0=Pb[pb][:, 0:1], scalar1=mL),
                           reads=[f"Pb{pb}", "vecs"], writes=[f"Pb{pb}"])
                    else:
                        op("vector", lambda e, pb=pb, mR=mR: e.tensor_scalar_mul(out=Pb[pb][:, 513:514], in0=Pb[pb][:, 513:514], scalar1=mR),
                           reads=[f"Pb{pb}", "vecs"], writes=[f"Pb{pb}"])

                def ffn_C(u):
                    cc, which = units[u]
                    pb = u % 2
                    cb = 6 + which
                    for j in range(3):
                        op("tensor", lambda e, pb=pb, j=j, cb=cb: e.matmul(
                            ps[:, cb, :], lhsT=d3[pb][:, j, :], rhs=Pb[pb][:, j:j + 512], start=(j == 0), stop=(j == 2)),
                           reads=[f"d3_{pb}", f"Pb{pb}"], writes=[pk(cb)])
                    if which == 1:
                        gi = cc % 2
                        op("scalar", lambda e, gi=gi, cc=cc: e.activation(out=gl[gi], in_=ps[:, 7, :], func=AF.Gelu,
                                                                          bias=vecs[:, V_FB + NFC + cc:V_FB + NFC + cc + 1]),
                           reads=[pk(7), "vecs"], writes=[f"gl{gi}"])
                        op("vector", lambda e, gi=gi, cc=cc: e.scalar_tensor_tensor(
                            out=GT[:, cc, :], in0=ps[:, 6, :], scalar=vecs[:, V_FB + cc:V_FB + cc + 1], in1=gl[gi], op0=ALU.add, op1=ALU.mult),
                           reads=[pk(6), "vecs", f"gl{gi}"], writes=[f"GT{cc}"])

                ffn_P(0)
                for u in range(NU):
                    if u + 1 < NU:
                        ffn_P(u + 1)
                    if u in (4, 12, 20):
                        dload((u - 4) // 8)
                    ffn_C(u)
                    if pending_ln2 and u >= 2 and u % 2 == 0:
                        pending_ln2.pop(0)()
                while pending_ln2:
                    pending_ln2.pop(0)()
                if gq == 0:
                    group_head(1)
                xs = slice(16 + gq * 512, 16 + (gq + 1) * 512)
                for f in range(8):
                    db = f % 2
                    wdb = f % 3
                    bank = 2 + (f % 2)
                    for cc in range(NFC):
                        op("tensor", lambda e, wdb=wdb, cc=cc, bank=bank: e.matmul(
                            ps[:, bank, :], lhsT=Wd[wdb][:, cc, :], rhs=GT[:, cc, :], start=(cc == 0), stop=(cc == NFC - 1)),
                           reads=[f"Wd{wdb}", f"GT{cc}"], writes=[pk(bank)])
                    if f + 3 < 8:
                        dload(f + 3)
                    op("scalar", lambda e, f=f, db=db, bank=bank: e.activation(out=g2t[db], in_=ps[:, bank, :], func=AF.Identity,
                                                                               scale=mcol(M_G2 + f)),
                       reads=[pk(bank), "mod"], writes=[f"g2t{db}"])
                    op("vector", lambda e, f=f, db=db, xs=xs: e.scalar_tensor_tensor(out=y2[:, f, :], in0=X1[:, f, xs], scalar=ALPHA, in1=g2t[db],
                                                                                     op0=ALU.mult, op1=ALU.add),
                       reads=["X1", f"g2t{db}"], writes=[f"y2_{f}"])
                srcs = [(y2[:, f, :], f"y2_{f}") for f in range(8)]
                t0 = h * HALF + gq * 512

                def ln2_out(f, srcs=srcs, t0=t0):
                    oi = f % 3
                    ln_apply([srcs[f]], [(ob[oi], f"ob{oi}")], 512, [vecs[:, V_LN2G + f:V_LN2G + f + 1]], [vecs[:, V_LN2B + f:V_LN2B + f + 1]], ["vecs"])
                    P.dma("sync", sl_out[oi], [(outT_v[:, f, t0:t0 + 512], ob[oi])], reads=[f"ob{oi}"])

                if gq == 1:
                    ln_stats(srcs, 512, ones1024, "ones1024", 4, 5)
                    for f in range(8):
                        ln2_out(f)
                else:
                    idxs = {}

                    def sl_E(k, srcs=srcs, idxs=idxs):
                        idxs[k] = ln_stats_E(srcs[k][0], srcs[k][1], 512)

                    def sl_M(k, idxs=idxs):
                        ln_stats_M(idxs[k], k, 8, 512, ones1024, "ones1024", 0, 1)

                    pending_ln2.append(lambda: sl_E(0))
                    for k in range(1, 8):
                        pending_ln2.append(lambda k=k: (sl_M(k - 1), sl_E(k)))
                    pending_ln2.append(lambda: sl_M(7))
                    pending_ln2.append(lambda: ln_stats_fin(512, 0, 1))
                    for f in range(8):
                        pending_ln2.append(lambda f=f, fn=ln2_out: fn(f))
            P.barrier()
            AR.top = hmark

        for sl in P.all_slots:
            if sl.count > 0:
                P.prog["sync"].append(lambda e, s=sl.sem, v=sl.count: e.wait_ge(s, v))
        P.emit()
    return nc


_PROG_CACHE = {}


def _rope_tables(tok):
    tok = np.asarray(tok)
    inv = (10000.0 ** (-np.arange(16, dtype=np.float32) / 16.0)).astype(np.float32)
    rows = (tok // 64).astype(np.float32)
    cols = (tok % 64).astype(np.float32)
    ang_r = rows[None, :] * inv[:, None]
    ang_c = cols[None, :] * inv[:, None]
    C = np.zeros((64, len(tok)), np.float32)
    S = np.zeros((64, len(tok)), np.float32)
    C[0:16] = np.cos(ang_r); C[16:32] = np.cos(ang_r); C[32:48] = np.cos(ang_c); C[48:64] = np.cos(ang_c)
    S[0:16] = -np.sin(ang_r); S[16:32] = np.sin(ang_r); S[32:48] = -np.sin(ang_c); S[48:64] = np.sin(ang_c)
    return np.concatenate([C, C], 0), np.concatenate([S, S], 0)


def _partner():
    d = np.arange(64)
    return np.where((d % 32) < 16, d + 16, d - 16)


def _col(v):
    v = np.asarray(v, np.float32)
    return np.ascontiguousarray(v.reshape(-1, 128).T)


def kernel(x, c, w_ada, b_ada, w_in, q_norm_g, k_norm_g, conv_dw_w, conv_dw_b, conv_ln_g, conv_ln_b, w_conv_pw2,
           attn_out_g, conv_out_g, w_o, ln1_g, ln1_b, w_up, ffn_dw_w, ffn_dw_b, w_down, ln2_g, ln2_b, _debug=False):
    f32 = np.float32
    x = np.asarray(x, f32); c = np.asarray(c, f32)
    w_ada = np.asarray(w_ada, f32)[0]; b_ada = np.asarray(b_ada, f32)[0]; w_in = np.asarray(w_in, f32)[0]
    qg = np.asarray(q_norm_g, f32)[0]; kg = np.asarray(k_norm_g, f32)[0]
    dw_w = np.asarray(conv_dw_w, f32)[0]; dw_b = np.asarray(conv_dw_b, f32)[0]
    cln_g = np.asarray(conv_ln_g, f32)[0]; cln_b = np.asarray(conv_ln_b, f32)[0]
    pw2 = np.asarray(w_conv_pw2, f32)[0]; aog = np.asarray(attn_out_g, f32)[0]; cog = np.asarray(conv_out_g, f32)[0]
    w_o = np.asarray(w_o, f32)[0]; w_up = np.asarray(w_up, f32)[0]; w_down = np.asarray(w_down, f32)[0]
    fw = np.asarray(ffn_dw_w, f32)[0]; fb = np.asarray(ffn_dw_b, f32)[0]
    l1g = np.asarray(ln1_g, f32)[0]; l1b = np.asarray(ln1_b, f32)[0]; l2g = np.asarray(ln2_g, f32)[0]; l2b = np.asarray(ln2_b, f32)[0]

    part = _partner()
    wq = w_in[:, 0:512].reshape(D, 8, 64)
    wk = w_in[:, 512:640].reshape(D, 2, 64)
    wv = w_in[:, 640:768]
    wglu = w_in[:, 768:1792]
    head_order = [0, 4, 1, 5, 2, 6, 3, 7]
    wq_p = wq[:, head_order, :].reshape(D, 512)
    wq_sw = wq[:, head_order, :][:, :, part].reshape(D, 512)
    wk_p = wk.reshape(D, 128)
    wk_sw = wk[:, :, part].reshape(D, 128)
    w_in_cat = np.concatenate([wq_p, wq_sw, wglu, wk_p, wk_sw, wv], axis=1)
    w_in_p = np.ascontiguousarray(w_in_cat.reshape(8, 128, 2432).transpose(1, 0, 2))
    w_ada_p = np.ascontiguousarray(w_ada.reshape(8, 128, 6, 1024).transpose(2, 1, 0, 3))
    w_pw2_p = np.ascontiguousarray(pw2.reshape(4, 128, 512).transpose(1, 0, 2))
    rows = np.concatenate([np.concatenate([np.arange(j * 64, (j + 1) * 64), np.arange((4 + j) * 64, (5 + j) * 64)]) for j in range(4)]
                          + [np.arange(512, 1024)])
    w_o_p = np.ascontiguousarray(w_o[rows, :].reshape(8, 128, 1024).transpose(1, 0, 2))
    w_up_p = np.ascontiguousarray(w_up.reshape(8, 128, 44, 128).transpose(2, 1, 0, 3))
    w_down_p = np.ascontiguousarray(w_down.reshape(NFC, 128, 8, 128).transpose(2, 1, 0, 3))

    consts = np.zeros((128, 256), f32)
    consts[:, 0:128] = np.eye(128, dtype=f32)
    consts[0:64, 128:192] = 1.0 / 64.0
    consts[64:128, 192:256] = 1.0 / 64.0

    ropekC, ropekS = _rope_tables(np.arange(SEQ))
    ropek = np.ascontiguousarray(np.stack([ropekC, ropekS]))

    def vec_common():
        v = np.zeros((128, NV), f32)
        v[:, V_BADA:V_BADA + 48] = _col(b_ada)
        g2 = np.concatenate([qg, qg]); v[:, V_QG] = g2; v[:, V_QG + 1] = np.concatenate([qg[part], qg[part]])
        k2 = np.concatenate([kg, kg]); v[:, V_KG] = k2; v[:, V_KG + 1] = np.concatenate([kg[part], kg[part]])
        v[:, V_DWB:V_DWB + 4] = _col(dw_b)
        v[:, V_CLNG:V_CLNG + 4] = _col(cln_g)
        v[:, V_CLNB:V_CLNB + 4] = _col(cln_b)
        v[:, V_COG:V_COG + 4] = _col(cog.reshape(-1))
        for j in range(4):
            v[0:64, V_AOG + j] = aog[j]
            v[64:128, V_AOG + j] = aog[4 + j]
        v[:, V_LN1G:V_LN1G + 8] = _col(l1g); v[:, V_LN1B:V_LN1B + 8] = _col(l1b)
        v[:, V_LN2G:V_LN2G + 8] = _col(l2g); v[:, V_LN2B:V_LN2B + 8] = _col(l2b)
        v[:, V_FB:V_FB + 44] = _col(fb)
        for j in range(3):
            v[:, V_FW + j * 44:V_FW + (j + 1) * 44] = _col(fw[j])
        for i in range(4):
            v[:, V_DW + i * 31:V_DW + (i + 1) * 31] = dw_w[:, i * 128:(i + 1) * 128].T
        return v

    vcommon = vec_common()
    in_maps = []
    for ci in range(NCORE):
        b, r = divmod(ci, 4)
        start = r * OWN
        xb_T = np.ascontiguousarray(x[b].T)
        xown = np.zeros((2, D, HT), f32)
        rq = np.zeros((2, 2, 128, HT), f32)
        v = vcommon.copy()
        v[:, V_C:V_C + 8] = _col(c[b])
        for h in range(2):
            t0 = start + h * HALF - 16
            tok = np.arange(t0, t0 + HT)
            valid = (tok >= 0) & (tok < SEQ)
            xown[h][:, valid] = xb_T[:, tok[valid]]
            Cq, Sq = _rope_tables(np.clip(tok, 0, SEQ - 1))
            rq[h, 0] = Cq; rq[h, 1] = Sq
            v[:, V_MASK + 2 * h] = 1.0 if t0 + 15 >= 0 else 0.0
            v[:, V_MASK + 2 * h + 1] = 1.0 if t0 + 16 + HALF < SEQ else 0.0
        in_maps.append({
            "xT_all": xb_T, "xT_own": xown, "vecs": v, "w_ada_p": w_ada_p, "w_in_p": w_in_p, "ropeq": rq, "ropek": ropek,
            "consts": consts, "w_pw2_p": w_pw2_p, "w_o_p": w_o_p, "w_up_p": w_up_p, "w_down_p": w_down_p,
        })

    key = bool(_debug)
    if key not in _PROG_CACHE:
        _PROG_CACHE[key] = build_program(debug=key)
    nc = _PROG_CACHE[key]
    res = run_bass_kernel_spmd(nc, in_maps, core_ids=list(range(NCORE)))
    out = np.empty((NB, SEQ, D), f32)
    for ci in range(NCORE):
        b, r = divmod(ci, 4)
        out[b, r * OWN:(r + 1) * OWN, :] = res.results[ci]["outT"].T
    if _debug:
        return out, [res.results[ci]["dbg"] for ci in range(NCORE)]
    return out
```

```python
import contextlib
import numpy as np
import concourse.bass as bass
import concourse.mybir as mybir
from concourse.bass_utils import run_bass_kernel_spmd

F32 = mybir.dt.float32
BF16 = mybir.dt.bfloat16
ALU = mybir.AluOpType
AF = mybir.ActivationFunctionType

ENGS = ("tensor", "vector", "scalar", "gpsimd", "sync")

D = 1024
SEQ = 8192
NB = 2
NCORE = 8
OWN = 2048
HALF = 1024
HT = 1056
NCH = 3
CW = 352
HPAD = 15
DFF = 2816
NFC = 22
ALPHA = 2.0 ** 0.25
EPS = 1e-6
NKT = SEQ // 128
TPS = 2

V_C = 0; V_BADA = 8; V_QG = 56; V_KG = 58; V_DWB = 60; V_CLNG = 64; V_CLNB = 68; V_COG = 72; V_AOG = 76
V_LN1G = 80; V_LN1B = 88; V_LN2G = 96; V_LN2B = 104; V_FB = 112; V_FW = 156; V_DW = 288; V_MASK = 412; NV = 416
M_SH1 = 0; M_SC1 = 8; M_G1 = 16; M_SH2 = 24; M_SC2 = 32; M_G2 = 40; M_SC1P = 48; M_SC2P = 56


class DmaSlot:
    def __init__(self, sem):
        self.sem = sem
        self.count = 0


class Prog:
    def __init__(self, nc, stack):
        self.nc = nc
        self.stack = stack
        self.sem = {e: stack.enter_context(nc.semaphore("sem_" + e)) for e in ENGS}
        self.cnt = {e: 0 for e in ENGS}
        self.prog = {e: [] for e in ENGS}
        self.waited = {e: {} for e in ENGS}
        self.lastw = {}
        self.rds = {}
        self.all_slots = []

    def slot(self, name=None):
        s = DmaSlot(self.stack.enter_context(self.nc.semaphore(name or f"dsl{len(self.all_slots)}")))
        self.all_slots.append(s)
        return s

    def _need(self, eng, deps):
        best = {}
        for (s, v, src) in deps:
            if src == eng and eng == "tensor":
                continue
            k = id(s)
            if v > self.waited[eng].get(k, 0) and v > best.get(k, (None, 0))[1]:
                best[k] = (s, v)
        for k, (s, v) in best.items():
            self.waited[eng][k] = v
            self.prog[eng].append(lambda e, s=s, v=v: e.wait_ge(s, v))

    def _deps(self, reads, writes):
        deps = []
        for r in reads:
            if r in self.lastw:
                deps.append(self.lastw[r])
        for w in writes:
            if w in self.lastw:
                deps.append(self.lastw[w])
            deps.extend(self.rds.get(w, {}).values())
        return deps

    def _record(self, rec, reads, writes):
        for r in reads:
            self.rds.setdefault(r, {})[id(rec[0])] = rec
        for w in writes:
            self.lastw[w] = rec
            self.rds[w] = {}

    def op(self, eng, fn, reads=(), writes=()):
        self._need(eng, self._deps(reads, writes))
        self.cnt[eng] += 1
        v = self.cnt[eng]
        s = self.sem[eng]
        self.prog[eng].append(lambda e, fn=fn, s=s: fn(e).then_inc(s, 1))
        self._record((s, v, eng), reads, writes)

    def dma(self, q, slot, items, reads=(), writes=()):
        self._need(q, self._deps(reads, writes))
        for (o, i) in items:
            slot.count += 16
            self.prog[q].append(lambda e, o=o, i=i, s=slot.sem: e.dma_start(out=o, in_=i).then_inc(s, 16))
        self._record((slot.sem, slot.count, None), reads, writes)

    def barrier(self):
        deps = [(self.sem[f], self.cnt[f], f) for f in ENGS if self.cnt[f] > 0]
        deps += [(sl.sem, sl.count, None) for sl in self.all_slots if sl.count > 0]
        for e in ENGS:
            self._need(e, [d for d in deps if d[2] != e])

    def emit(self):
        nc = self.nc
        with nc.Block() as block:
            @block.sync
            def _(e):
                for f in self.prog["sync"]:
                    f(e)

            @block.tensor
            def _(e):
                for f in self.prog["tensor"]:
                    f(e)

            @block.vector
            def _(e):
                for f in self.prog["vector"]:
                    f(e)

            @block.scalar
            def _(e):
                for f in self.prog["scalar"]:
                    f(e)

            @block.gpsimd
            def _(e):
                for f in self.prog["gpsimd"]:
                    f(e)


class Arena:
    def __init__(self, ap32, ncols32):
        self.ap = ap32
        self.nbytes = ncols32 * 4
        self.limit = self.nbytes
        self.top = 0

    def alloc(self, free_shape, dtype):
        esz = 4 if dtype == F32 else 2
        n = 1
        for s in free_shape:
            n *= s
        nb = (n * esz + 63) // 64 * 64
        off = self.top
        self.top += nb
        assert self.top <= self.limit, f"arena overflow {self.top} > {self.limit}"
        v = self.ap[:, off // 4:(off + nb) // 4]
        if dtype != F32:
            v = v.bitcast(dtype)
        v = v[:, 0:n]
        if len(free_shape) == 2:
            v = v.rearrange("p (a b) -> p a b", a=free_shape[0])
        elif len(free_shape) == 3:
            v = v.rearrange("p (a b c) -> p a b c", a=free_shape[0], b=free_shape[1])
        return v


def build_program(debug=False):
    nc = bass.Bass("TRN2", target_bir_lowering=False)

    def din(name, shape):
        return nc.dram_tensor(name, list(shape), F32, kind="ExternalInput").ap()

    xT_all = din("xT_all", [D, SEQ])
    xT_own = din("xT_own", [2, D, HT])
    vecs_d = din("vecs", [128, NV])
    w_ada_d = din("w_ada_p", [6, 128, 8, 1024])
    w_in_d = din("w_in_p", [128, 8, 2432])
    ropeq_d = din("ropeq", [2, 2, 128, HT])
    ropek_d = din("ropek", [2, 128, SEQ])
    consts_d = din("consts", [128, 256])
    pw2_d = din("w_pw2_p", [128, 4, 512])
    wo_d = din("w_o_p", [128, 8, 1024])
    wup_d = din("w_up_p", [44, 128, 8, 128])
    wdn_d = din("w_down_p", [8, 128, NFC, 128])
    outT = nc.dram_tensor("outT", [D, OWN], F32, kind="ExternalOutput").ap()
    dbg = None
    if debug:
        dbg = nc.dram_tensor("dbg", [128, 8, HT], F32, kind="ExternalOutput").ap()

    xT_all_v = xT_all.rearrange("(kc p) t -> p kc t", p=128)
    outT_v = outT.rearrange("(kc p) t -> p kc t", p=128)

    NA = 51200
    with contextlib.ExitStack() as st:
        P = Prog(nc, st)
        arena_t = st.enter_context(nc.sbuf_tensor("arena", [128, NA], F32))
        ps = st.enter_context(nc.psum_tensor("ps", [128, 8, 512], F32))
        AR = Arena(arena_t[:, :], NA)

        def pk(b):
            return f"ps{b}"

        op = P.op

        vecs = AR.alloc([NV], F32)
        consts = AR.alloc([256], F32)
        mod = AR.alloc([64], F32)
        ident = AR.alloc([128], BF16)
        ones1024 = AR.alloc([128], BF16)
        ones512 = AR.alloc([128], BF16)
        c_act = AR.alloc([8], BF16)
        kT = AR.alloc([SEQ], BF16)
        Vx = AR.alloc([NKT, 192], BF16)
        blk = consts[:, 128:256]
        xb = [AR.alloc([512], BF16) for _ in range(3)]
        sqb = [AR.alloc([512], BF16) for _ in range(3)]
        s_mean = AR.alloc([512], F32)
        s_var = AR.alloc([512], F32)
        s_rstd = AR.alloc([512], F32)
        tAB = AR.alloc([2048], F32)
        tA = [tAB[:, i * 512:(i + 1) * 512] for i in range(2)]
        tB = [tAB[:, (2 + i) * 512:(3 + i) * 512] for i in range(2)]
        tAB_bf = tAB.bitcast(BF16)
        r_sq = AR.alloc([512], F32)
        r_v = AR.alloc([512], F32)
        r_r = AR.alloc([512], F32)
        r_t1 = AR.alloc([512], F32)
        r_t2 = AR.alloc([512], F32)
        sc_bf = AR.alloc([8], BF16)
        sh_bf = AR.alloc([8], BF16)
        ones_row = AR.alloc([512], BF16)
        mr_row = [AR.alloc([512], BF16) for _ in range(2)]
        base_top = AR.top

        sl_const = P.slot("sl_const")
        P.dma("sync", sl_const, [(vecs, vecs_d), (consts, consts_d)], writes=["vecs", "consts"])
        op("vector", lambda e: e.tensor_copy(out=ident, in_=consts[:, 0:128]), reads=["consts"], writes=["ident"])
        op("vector", lambda e: e.memset(ones1024, 1.0 / 1024.0), writes=["ones1024"])
        op("vector", lambda e: e.memset(ones512, 1.0 / 512.0), writes=["ones512"])
        op("gpsimd", lambda e: e.memset(Vx[:, :, 64:128], 1.0), writes=["Vx"])

        uid = [0]

        def cnt():
            uid[0] += 1
            return uid[0]

        def ln_stats(srcs, T, ones, ones_key, bm, bx):
            n = len(srcs)
            for k, (ap, key) in enumerate(srcs):
                i = cnt() % 3
                if k % 2 == 0:
                    op("vector", lambda e, ap=ap, i=i: e.tensor_copy(out=xb[i][:, :T], in_=ap), reads=[key], writes=[f"xb{i}"])
                else:
                    op("scalar", lambda e, ap=ap, i=i: e.activation(out=xb[i][:, :T], in_=ap, func=AF.Identity), reads=[key], writes=[f"xb{i}"])
                op("scalar", lambda e, ap=ap, i=i: e.activation(out=sqb[i][:, :T], in_=ap, func=AF.Square),
                   reads=[key], writes=[f"sqb{i}"])
                op("tensor", lambda e, i=i, k=k: e.matmul(ps[:, bm, :T], lhsT=ones, rhs=xb[i][:, :T], start=(k == 0), stop=(k == n - 1)),
                   reads=[f"xb{i}", ones_key], writes=[pk(bm)])
                op("tensor", lambda e, i=i, k=k: e.matmul(ps[:, bx, :T], lhsT=ones, rhs=sqb[i][:, :T], start=(k == 0), stop=(k == n - 1)),
                   reads=[f"sqb{i}", ones_key], writes=[pk(bx)])
            op("scalar", lambda e: e.activation(out=s_mean[:, :T], in_=ps[:, bm, :T], func=AF.Identity), reads=[pk(bm)], writes=["s_mean"])
            op("scalar", lambda e: e.activation(out=s_var[:, :T], in_=ps[:, bm, :T], func=AF.Square), reads=[pk(bm)], writes=["s_var"])
            op("vector", lambda e: e.tensor_tensor(out=s_var[:, :T], in0=ps[:, bx, :T], in1=s_var[:, :T], op=ALU.subtract),
               reads=[pk(bx), "s_var"], writes=["s_var"])
            op("vector", lambda e: e.tensor_scalar(out=s_var[:, :T], in0=s_var[:, :T], scalar1=0.0, scalar2=EPS, op0=ALU.max, op1=ALU.add),
               reads=["s_var"], writes=["s_var"])
            op("scalar", lambda e: e.activation(out=s_rstd[:, :T], in_=s_var[:, :T], func=AF.Ln), reads=["s_var"], writes=["s_rstd"])
            op("scalar", lambda e: e.activation(out=s_rstd[:, :T], in_=s_rstd[:, :T], func=AF.Exp, scale=-0.5),
               reads=["s_rstd"], writes=["s_rstd"])

        def ln_stats_E(ap, key, T):
            i = cnt() % 3
            op("vector", lambda e, ap=ap, i=i: e.tensor_copy(out=xb[i][:, :T], in_=ap), reads=[key], writes=[f"xb{i}"])
            op("scalar", lambda e, ap=ap, i=i: e.activation(out=sqb[i][:, :T], in_=ap, func=AF.Square),
               reads=[key], writes=[f"sqb{i}"])
            return i

        def ln_stats_M(i, k, n, T, ones, ones_key, bm, bx):
            op("tensor", lambda e, i=i, k=k: e.matmul(ps[:, bm, :T], lhsT=ones, rhs=xb[i][:, :T], start=(k == 0), stop=(k == n - 1)),
               reads=[f"xb{i}", ones_key], writes=[pk(bm)])
            op("tensor", lambda e, i=i, k=k: e.matmul(ps[:, bx, :T], lhsT=ones, rhs=sqb[i][:, :T], start=(k == 0), stop=(k == n - 1)),
               reads=[f"sqb{i}", ones_key], writes=[pk(bx)])

        def ln_stats_fin(T, bm, bx):
            op("scalar", lambda e: e.activation(out=s_mean[:, :T], in_=ps[:, bm, :T], func=AF.Identity), reads=[pk(bm)], writes=["s_mean"])
            op("scalar", lambda e: e.activation(out=s_var[:, :T], in_=ps[:, bm, :T], func=AF.Square), reads=[pk(bm)], writes=["s_var"])
            op("vector", lambda e: e.tensor_tensor(out=s_var[:, :T], in0=ps[:, bx, :T], in1=s_var[:, :T], op=ALU.subtract),
               reads=[pk(bx), "s_var"], writes=["s_var"])
            op("vector", lambda e: e.tensor_scalar(out=s_var[:, :T], in0=s_var[:, :T], scalar1=0.0, scalar2=EPS, op0=ALU.max, op1=ALU.add),
               reads=["s_var"], writes=["s_var"])
            op("scalar", lambda e: e.activation(out=s_rstd[:, :T], in_=s_var[:, :T], func=AF.Ln), reads=["s_var"], writes=["s_rstd"])
            op("scalar", lambda e: e.activation(out=s_rstd[:, :T], in_=s_rstd[:, :T], func=AF.Exp, scale=-0.5),
               reads=["s_rstd"], writes=["s_rstd"])

        def ln_apply(srcs, dsts, T, scales, biases, sb_keys):
            for k, ((ap, key), (dap, dkey)) in enumerate(zip(srcs, dsts)):
                i = cnt() % 2
                op("vector", lambda e, ap=ap, i=i: e.tensor_tensor(out=tA[i][:, :T], in0=ap, in1=s_mean[:, :T], op=ALU.subtract),
                   reads=[key, "s_mean"], writes=[f"tA{i}"])
                op("vector", lambda e, i=i: e.tensor_tensor(out=tB[i][:, :T], in0=tA[i][:, :T], in1=s_rstd[:, :T], op=ALU.mult),
                   reads=[f"tA{i}", "s_rstd"], writes=[f"tB{i}"])
                op("scalar", lambda e, dap=dap, i=i, k=k: e.activation(out=dap, in_=tB[i][:, :T], func=AF.Identity,
                                                                      scale=scales[k], bias=biases[k]),
                   reads=[f"tB{i}"] + list(sb_keys), writes=[dkey])

        def ln_apply_fold(srcs, dsts, T, mrb):
            for k, ((ap, key), (dap, dkey)) in enumerate(zip(srcs, dsts)):
                op("vector", lambda e, ap=ap, dap=dap, k=k: e.scalar_tensor_tensor(
                    out=dap, in0=ap, scalar=mcol(M_SC1P + k), in1=s_rstd[:, :T], op0=ALU.mult, op1=ALU.mult),
                   reads=[key, "mod", "s_rstd"], writes=[dkey])
            op("vector", lambda e, mrb=mrb: e.tensor_tensor(out=mr_row[mrb][0:1, :T], in0=s_mean[0:1, :T], in1=s_rstd[0:1, :T], op=ALU.mult),
               reads=["s_mean", "s_rstd"], writes=[f"mr{mrb}"])

        def fold_rows(W, wkey, N, ncs_row, sb_row, rkey, bank):
            for n0 in range(0, N, 512):
                w = min(512, N - n0)
                for (col, row, sgn) in ((sc_bf, ncs_row, -1.0), (sh_bf, sb_row, 1.0)):
                    for kc in range(8):
                        op("tensor", lambda e, col=col, kc=kc, n0=n0, w=w: e.matmul(
                            ps[0:1, bank, :w], lhsT=col[:, kc:kc + 1], rhs=W[:, kc, n0:n0 + w], start=(kc == 0), stop=(kc == 7)),
                           reads=[wkey, "scsh"], writes=[pk(bank)])
                    op("scalar", lambda e, row=row, n0=n0, w=w, sgn=sgn: e.activation(
                        out=row[0:1, n0:n0 + w], in_=ps[0:1, bank, :w], func=AF.Identity, scale=sgn),
                       reads=[pk(bank)], writes=[rkey])

        def fold_mm(bank, T, ncs_row, sb_row, rkey, c0, mrb):
            op("tensor", lambda e: e.matmul(ps[:, bank, :T], lhsT=ncs_row[0:1, c0:c0 + 128], rhs=mr_row[mrb][0:1, :T], start=False, stop=False),
               reads=[rkey, f"mr{mrb}"], writes=[pk(bank)])
            op("tensor", lambda e: e.matmul(ps[:, bank, :T], lhsT=sb_row[0:1, c0:c0 + 128], rhs=ones_row[0:1, :T], start=False, stop=True),
               reads=[rkey, "ones_row"], writes=[pk(bank)])

        def rsqrt_from_psum(bank, T):
            op("vector", lambda e: e.tensor_scalar_add(out=r_v[:, :T], in0=ps[:, bank, :T], scalar1=EPS), reads=[pk(bank)], writes=["r_v"])
            op("scalar", lambda e: e.activation(out=r_v[:, :T], in_=r_v[:, :T], func=AF.Ln), reads=["r_v"], writes=["r_v"])
            op("scalar", lambda e: e.activation(out=r_r[:, :T], in_=r_v[:, :T], func=AF.Exp, scale=-0.5), reads=["r_v"], writes=["r_r"])

        def rope_rms(bA, bB, bR, gcol, C, S, ckeys, out, okey, T):
            op("scalar", lambda e: e.activation(out=r_sq[:, :T], in_=ps[:, bA, :T], func=AF.Square), reads=[pk(bA)], writes=["r_sq"])
            op("tensor", lambda e: e.matmul(ps[:, bR, :T], lhsT=blk, rhs=r_sq[:, :T], start=True, stop=True),
               reads=["r_sq", "consts"], writes=[pk(bR)])
            rsqrt_from_psum(bR, T)
            op("vector", lambda e: e.scalar_tensor_tensor(out=r_t1[:, :T], in0=ps[:, bA, :T], scalar=vecs[:, gcol:gcol + 1], in1=C,
                                                          op0=ALU.mult, op1=ALU.mult),
               reads=[pk(bA), "vecs"] + list(ckeys), writes=["r_t1"])
            op("vector", lambda e: e.scalar_tensor_tensor(out=r_t2[:, :T], in0=ps[:, bB, :T], scalar=vecs[:, gcol + 1:gcol + 2], in1=S,
                                                          op0=ALU.mult, op1=ALU.mult),
               reads=[pk(bB), "vecs"] + list(ckeys), writes=["r_t2"])
            op("vector", lambda e: e.tensor_tensor(out=r_t1[:, :T], in0=r_t1[:, :T], in1=r_t2[:, :T], op=ALU.add),
               reads=["r_t1", "r_t2"], writes=["r_t1"])
            if isinstance(out, list):
                for (p0, p1, oap) in out:
                    op("vector", lambda e, p0=p0, p1=p1, oap=oap: e.tensor_tensor(out=oap, in0=r_t1[p0:p1, :T], in1=r_r[p0:p1, :T], op=ALU.mult),
                       reads=["r_t1", "r_r"], writes=[okey])
            else:
                op("vector", lambda e: e.tensor_tensor(out=out, in0=r_t1[:, :T], in1=r_r[:, :T], op=ALU.mult),
                   reads=["r_t1", "r_r"], writes=[okey])

        def sigmoid_recip(src_ap, src_key, T, dst, dkey):
            op("scalar", lambda e: e.activation(out=dst, in_=src_ap, func=AF.Exp, scale=-1.0), reads=[src_key], writes=[dkey])
            op("vector", lambda e: e.tensor_scalar_add(out=dst, in0=dst, scalar1=1.0), reads=[dkey], writes=[dkey])
            op("vector", lambda e: e.reciprocal(out=dst, in_=dst), reads=[dkey], writes=[dkey])

        mark = AR.top
        wa = [AR.alloc([8, 1024], BF16) for _ in range(2)]
        sl_wa = [P.slot(f"sl_wa{i}") for i in range(2)]
        c_tmp = AR.alloc([8], F32)
        sigmoid_recip(vecs[:, V_C:V_C + 8], "vecs", 8, c_tmp, "c_tmp")
        op("vector", lambda e: e.tensor_tensor(out=c_act, in0=vecs[:, V_C:V_C + 8], in1=c_tmp, op=ALU.mult),
           reads=["vecs", "c_tmp"], writes=["c_act"])
        def mod_dma(part):
            b = part % 2
            P.dma("gpsimd", sl_wa[b], [(wa[b], w_ada_d[part])], writes=[f"wa{b}"])

        def mod_mm(part, bank):
            b = part % 2
            for nn in range(8):
                n = part * 8 + nn
                for kc in range(8):
                    op("tensor", lambda e, b=b, nn=nn, n=n, kc=kc: e.matmul(
                        ps[:, bank, n:n + 1], lhsT=wa[b][:, kc, nn * 128:(nn + 1) * 128], rhs=c_act[:, kc:kc + 1],
                        start=(kc == 0), stop=(kc == 7)),
                       reads=[f"wa{b}", "c_act"], writes=[pk(bank)])

        def mod_fin_rest():
            op("vector", lambda e: e.tensor_tensor(out=mod[:, 16:48], in0=ps[:, 7, 16:48], in1=vecs[:, V_BADA + 16:V_BADA + 48], op=ALU.add),
               reads=[pk(7), "vecs"], writes=["mod"])
            op("vector", lambda e: e.tensor_scalar_add(out=mod[:, M_SC2P:M_SC2P + 8], in0=mod[:, M_SC2:M_SC2 + 8], scalar1=1.0),
               reads=["mod"], writes=["mod"])

        mod_dma(0)
        mod_dma(1)
        mod_mm(0, 0)
        mod_mm(1, 0)
        op("vector", lambda e: e.tensor_tensor(out=mod[:, 0:16], in0=ps[:, 0, 0:16], in1=vecs[:, V_BADA:V_BADA + 16], op=ALU.add),
           reads=[pk(0), "vecs"], writes=["mod"])
        op("vector", lambda e: e.tensor_scalar_add(out=mod[:, M_SC1P:M_SC1P + 8], in0=mod[:, M_SC1:M_SC1 + 8], scalar1=1.0),
           reads=["mod"], writes=["mod"])
        op("vector", lambda e: e.tensor_copy(out=sc_bf, in_=mod[:, M_SC1P:M_SC1P + 8]), reads=["mod"], writes=["scsh"])
        op("vector", lambda e: e.tensor_copy(out=sh_bf, in_=mod[:, M_SH1:M_SH1 + 8]), reads=["mod"], writes=["scsh"])
        op("vector", lambda e: e.memset(ones_row, 1.0), writes=["ones_row"])
        P.barrier()
        mark0 = mark

        def mcol(c):
            return mod[:, c:c + 1]

        Wkv = AR.alloc([8, 384], BF16)
        xc = [AR.alloc([8, 512], F32) for _ in range(2)]
        rkC = [AR.alloc([512], F32) for _ in range(2)]
        rkS = [AR.alloc([512], F32) for _ in range(2)]
        uT = [AR.alloc([8, 512], BF16) for _ in range(2)]
        sl_wkv = P.slot("sl_wkv")
        sl_xc = [P.slot(f"sl_xc{i}") for i in range(2)]
        sl_rk = [P.slot(f"sl_rk{i}") for i in range(2)]
        P.dma("gpsimd", sl_wkv, [(Wkv, w_in_d[:, :, 2048:2432])], writes=["Wkv"])
        mod_dma(2)
        mod_dma(3)
        ncs_k = AR.alloc([384], BF16)
        sb_k = AR.alloc([384], BF16)
        def kv_A(ck):
            b = ck % 2
            cs = slice(ck * 512, (ck + 1) * 512)
            P.dma("sync", sl_xc[b], [(xc[b], xT_all_v[:, :, cs])], writes=[f"xc{b}"])
            P.dma("sync", sl_rk[b], [(rkC[b], ropek_d[0][:, cs]), (rkS[b], ropek_d[1][:, cs])], writes=[f"rk{b}"])
            srcs = [(xc[b][:, k, :], f"xc{b}") for k in range(8)]
            ln_stats(srcs, 512, ones1024, "ones1024", 0, 1)
            ln_apply_fold(srcs, [(uT[b][:, k, :], f"uT{b}") for k in range(8)], 512, b)

        def kv_B(ck):
            b = ck % 2
            cs = slice(ck * 512, (ck + 1) * 512)
            for (bank, c0) in ((2, 0), (3, 128)):
                for kc in range(8):
                    op("tensor", lambda e, bank=bank, c0=c0, kc=kc, b=b: e.matmul(
                        ps[:, bank, :], lhsT=Wkv[:, kc, c0:c0 + 128], rhs=uT[b][:, kc, :], start=(kc == 0), stop=False),
                       reads=["Wkv", f"uT{b}"], writes=[pk(bank)])
                fold_mm(bank, 512, ncs_k, sb_k, "rows_k", c0, b)
            for s4 in range(4):
                for kc in range(8):
                    op("tensor", lambda e, s4=s4, kc=kc, b=b: e.matmul(
                        ps[:, 4, s4 * 128:(s4 + 1) * 128], lhsT=uT[b][:, kc, s4 * 128:(s4 + 1) * 128], rhs=Wkv[:, kc, 256:384],
                        start=(kc == 0), stop=False),
                       reads=["Wkv", f"uT{b}"], writes=[pk(4)])
                op("tensor", lambda e, s4=s4, b=b: e.matmul(
                    ps[:, 4, s4 * 128:(s4 + 1) * 128], lhsT=mr_row[b][0:1, s4 * 128:(s4 + 1) * 128], rhs=ncs_k[0:1, 256:384], start=False, stop=False),
                   reads=["rows_k", f"mr{b}"], writes=[pk(4)])
                op("tensor", lambda e, s4=s4: e.matmul(
                    ps[:, 4, s4 * 128:(s4 + 1) * 128], lhsT=ones_row[0:1, 0:128], rhs=sb_k[0:1, 256:384], start=False, stop=True),
                   reads=["rows_k", "ones_row"], writes=[pk(4)])
            psv = ps[:, 4, :].rearrange("p (a b) -> p a b", a=4)
            op("scalar", lambda e, ck=ck: e.activation(out=Vx[:, ck * 4:(ck + 1) * 4, 0:64], in_=psv[:, :, 0:64], func=AF.Identity),
               reads=[pk(4)], writes=["Vx"])
            op("scalar", lambda e, ck=ck: e.activation(out=Vx[:, ck * 4:(ck + 1) * 4, 128:192], in_=psv[:, :, 64:128], func=AF.Identity),
               reads=[pk(4)], writes=["Vx"])
            rope_rms(2, 3, 5, V_KG, rkC[b], rkS[b], [f"rk{b}"], kT[:, cs], "kT", 512)

        NKC = SEQ // 512
        kv_A(0)
        fold_rows(Wkv, "Wkv", 384, ncs_k, sb_k, "rows_k", 6)
        for ck in range(NKC):
            if ck + 1 < NKC:
                kv_A(ck + 1)
            kv_B(ck)
            if ck == 4:
                mod_mm(2, 7)
                mod_dma(4)
            elif ck == 5:
                mod_mm(3, 7)
                mod_dma(5)
            elif ck == 9:
                mod_mm(4, 7)
            elif ck == 10:
                mod_mm(5, 7)
                mod_fin_rest()
        P.barrier()
        AR.top = mark0

        sl_out = [P.slot(f"sl_out{i}") for i in range(3)]
        sl_wown = P.slot("sl_wown")
        sl_rq = P.slot("sl_rq")
        sl_xo = [P.slot(f"sl_xo{i}") for i in range(2)]
        sl_pw2 = P.slot("sl_pw2")
        sl_wo = P.slot("sl_wo")
        sl_wd = [P.slot(f"sl_wd{i}") for i in range(3)]
        sl_dbg = P.slot("sl_dbg") if debug else None

        X1_BYTES = 8 * HT * 4
        X1 = arena_t[:, NA - 8 * HT:NA].rearrange("p (a b) -> p a b", a=8)
        sl_wu = [P.slot(f"sl_wu{i}") for i in range(4)]
        for h in range(2):
            hmark = AR.top
            AR.limit = AR.nbytes
            qTp = AR.alloc([8, HT], BF16)
            AT = AR.alloc([8, HT], BF16)
            hp_mark = AR.top
            Hp = AR.alloc([4, HT + 2 * HPAD], BF16)
            xT_own_v = xT_own[h].rearrange("(kc p) t -> p kc t", p=128)
            mL = vecs[:, V_MASK + 2 * h:V_MASK + 2 * h + 1]
            mR = vecs[:, V_MASK + 2 * h + 1:V_MASK + 2 * h + 2]

            mark = AR.top
            Wown = AR.alloc([8, 2048], BF16)
            rqC = AR.alloc([HT], F32)
            rqS = AR.alloc([HT], F32)
            xo = [AR.alloc([8, CW], F32) for _ in range(2)]
            uo = [AR.alloc([8, CW], BF16) for _ in range(2)]
            sg = [AR.alloc([CW], F32) for _ in range(2)]
            P.dma("gpsimd", sl_wown, [(Wown, w_in_d[:, :, 0:2048])], writes=["Wown"])
            ncs_o = tAB_bf[:, 0:2048]
            sb_o = tAB_bf[:, 2048:4096]
            P.dma("sync", sl_rq, [(rqC, ropeq_d[h, 0]), (rqS, ropeq_d[h, 1])], writes=["rq"])
            op("gpsimd", lambda e: e.memset(Hp[:, :, 0:HPAD], 0.0), writes=["Hp"])
            op("gpsimd", lambda e: e.memset(Hp[:, :, HPAD + HT:HT + 2 * HPAD], 0.0), writes=["Hp"])
            op("gpsimd", lambda e: e.memset(qTp, 0.0), writes=["qT"])

            def own_A(c, xT_own_v=xT_own_v):
                b = c % 2
                cs = slice(c * CW, (c + 1) * CW)
                P.dma("sync", sl_xo[b], [(xo[b], xT_own_v[:, :, cs])], writes=[f"xo{b}"])
                srcs = [(xo[b][:, k, :], f"xo{b}") for k in range(8)]
                ln_stats(srcs, CW, ones1024, "ones1024", 0, 1)
                ln_apply_fold(srcs, [(uo[b][:, k, :], f"uo{b}") for k in range(8)], CW, b)

            def own_B(c):
                b = c % 2
                cs = slice(c * CW, (c + 1) * CW)
                for j in range(4):
                    bq = 2 + 2 * (j % 2)
                    for (bank, c0) in ((bq, j * 128), (bq + 1, 512 + j * 128)):
                        for kc in range(8):
                            op("tensor", lambda e, bank=bank, c0=c0, kc=kc, b=b: e.matmul(
                                ps[:, bank, :CW], lhsT=Wown[:, kc, c0:c0 + 128], rhs=uo[b][:, kc, :], start=(kc == 0), stop=False),
                               reads=["Wown", f"uo{b}"], writes=[pk(bank)])
                        fold_mm(bank, CW, ncs_o, sb_o, "rows_o", c0, b)
                    rope_rms(bq, bq + 1, 6, V_QG, rqC[:, cs], rqS[:, cs], ["rq"],
                             [(0, 64, qTp[0:64, 2 * j, cs]), (64, 128, qTp[64:128, 2 * j + 1, cs])], "qT", CW)
                for i in range(4):
                    bq = 2 + 2 * (i % 2)
                    for (bank, c0) in ((bq, 1024 + i * 128), (bq + 1, 1536 + i * 128)):
                        for kc in range(8):
                            op("tensor", lambda e, bank=bank, c0=c0, kc=kc, b=b: e.matmul(
                                ps[:, bank, :CW], lhsT=Wown[:, kc, c0:c0 + 128], rhs=uo[b][:, kc, :], start=(kc == 0), stop=False),
                               reads=["Wown", f"uo{b}"], writes=[pk(bank)])
                        fold_mm(bank, CW, ncs_o, sb_o, "rows_o", c0, b)
                    si = i % 2
                    sigmoid_recip(ps[:, bq + 1, :CW], pk(bq + 1), CW, sg[si], f"sg{si}")
                    op("vector", lambda e, bq=bq, i=i, si=si, c=c: e.tensor_tensor(
                        out=Hp[:, i, HPAD + c * CW:HPAD + (c + 1) * CW], in0=ps[:, bq, :CW], in1=sg[si], op=ALU.mult),
                       reads=[pk(bq), f"sg{si}"], writes=["Hp"])

            own_A(0)
            own_A(1)
            fold_rows(Wown, "Wown", 2048, ncs_o, sb_o, "rows_o", 7)
            for c in range(NCH):
                own_B(c)
                if c + 2 < NCH:
                    own_A(c + 2)
            op("vector", lambda e, mL=mL: e.tensor_scalar_mul(out=Hp[:, :, HPAD:HPAD + 16], in0=Hp[:, :, HPAD:HPAD + 16], scalar1=mL),
               reads=["Hp", "vecs"], writes=["Hp"])
            op("vector", lambda e, mR=mR: e.tensor_scalar_mul(out=Hp[:, :, HPAD + HT - 16:HPAD + HT], in0=Hp[:, :, HPAD + HT - 16:HPAD + HT], scalar1=mR),
               reads=["Hp", "vecs"], writes=["Hp"])
            P.barrier()
            AR.top = mark

            mark = AR.top
            dg = AR.alloc([4, 31, 128], BF16)
            pw2 = AR.alloc([4, 512], BF16)
            hcv = [AR.alloc([4, CW], F32) for _ in range(2)]
            zf = [AR.alloc([CW], F32) for _ in range(2)]
            ze = [AR.alloc([CW], F32) for _ in range(2)]
            zT = AR.alloc([4, CW], BF16)
            P.dma("gpsimd", sl_pw2, [(pw2, pw2_d)], writes=["pw2"])
            for i in range(4):
                for j in range(31):
                    col = V_DW + i * 31 + j
                    if j % 2 == 0:
                        op("vector", lambda e, i=i, j=j, col=col: e.tensor_scalar_mul(out=dg[:, i, j, :], in0=ident, scalar1=vecs[:, col:col + 1]),
                           reads=["ident", "vecs"], writes=[f"dg{i}_{j}"])
                    else:
                        op("scalar", lambda e, i=i, j=j, col=col: e.activation(out=dg[:, i, j, :], in_=ident, func=AF.Identity,
                                                                               scale=vecs[:, col:col + 1]),
                           reads=["ident", "vecs"], writes=[f"dg{i}_{j}"])

            def conv_A(c):
                hb = c % 2
                for i in range(4):
                    bank = 2 + (i % 2)
                    for j in range(31):
                        op("tensor", lambda e, i=i, j=j, bank=bank, c=c: e.matmul(
                            ps[:, bank, :CW], lhsT=dg[:, i, j, :], rhs=Hp[:, i, c * CW + j:c * CW + j + CW], start=(j == 0), stop=(j == 30)),
                           reads=[f"dg{i}_{j}", "Hp"], writes=[pk(bank)])
                    op("scalar", lambda e, i=i, bank=bank, hb=hb: e.activation(out=hcv[hb][:, i, :], in_=ps[:, bank, :CW], func=AF.Identity,
                                                                              bias=vecs[:, V_DWB + i:V_DWB + i + 1]),
                       reads=[pk(bank), "vecs"], writes=[f"hcv{hb}_{i}"])

            def conv_B(c):
                hb = c % 2
                srcs = [(hcv[hb][:, i, :], f"hcv{hb}_{i}") for i in range(4)]
                ln_stats(srcs, CW, ones512, "ones512", 0, 1)
                for i in range(4):
                    zi = i % 2
                    ln_apply([srcs[i]], [(zf[zi], f"zf{zi}")], CW, [vecs[:, V_CLNG + i:V_CLNG + i + 1]], [vecs[:, V_CLNB + i:V_CLNB + i + 1]], ["vecs"])
                    sigmoid_recip(zf[zi], f"zf{zi}", CW, ze[zi], f"ze{zi}")
                    op("vector", lambda e, i=i, zi=zi: e.tensor_tensor(out=zT[:, i, :], in0=zf[zi], in1=ze[zi], op=ALU.mult),
                       reads=[f"zf{zi}", f"ze{zi}"], writes=[f"zT{i}"])
                for o in range(4):
                    bank = 4 + (o % 2)
                    for i in range(4):
                        op("tensor", lambda e, o=o, i=i, bank=bank: e.matmul(
                            ps[:, bank, :CW], lhsT=pw2[:, i, o * 128:(o + 1) * 128], rhs=zT[:, i, :], start=(i == 0), stop=(i == 3)),
                           reads=["pw2", f"zT{i}"], writes=[pk(bank)])
                    op("scalar", lambda e, bank=bank: e.activation(out=r_sq[:, :CW], in_=ps[:, bank, :CW], func=AF.Square),
                       reads=[pk(bank)], writes=["r_sq"])
                    op("tensor", lambda e: e.matmul(ps[:, 6, :CW], lhsT=blk, rhs=r_sq[:, :CW], start=True, stop=True),
                       reads=["r_sq", "consts"], writes=[pk(6)])
                    rsqrt_from_psum(6, CW)
                    op("vector", lambda e, o=o, bank=bank, c=c: e.scalar_tensor_tensor(
                        out=AT[:, 4 + o, c * CW:(c + 1) * CW], in0=ps[:, bank, :CW], scalar=vecs[:, V_COG + o:V_COG + o + 1], in1=r_r[:, :CW],
                        op0=ALU.mult, op1=ALU.mult),
                       reads=[pk(bank), "vecs", "r_r"], writes=["AT"])

            conv_A(0)
            for c in range(NCH):
                if c + 1 < NCH:
                    conv_A(c + 1)
                conv_B(c)
            P.barrier()
            AR.top = hp_mark

            AR.limit = AR.nbytes - X1_BYTES
            mark = AR.top
            CA = 342
            NSB = 3 if TPS == 2 else 2
            LA = NSB - 1
            wo = AR.alloc([8, 1024], BF16)
            pT = [AR.alloc([TPS, CW], BF16) for _ in range(NSB)]
            rec = AR.alloc([CW], F32)
            On = AR.alloc([CW], F32)
            xr = AR.alloc([8, CW], F32)
            yb = AR.alloc([8, CW], F32)
            gt = [AR.alloc([CW], F32) for _ in range(2)]
            P.dma("gpsimd", sl_wo, [(wo, wo_d)], writes=["wo"])
            groups = [list(range(t0, min(t0 + TPS, NKT))) for t0 in range(0, NKT, TPS)]
            carry = []
            for c in range(NCH):
                cs = slice(15 + c * CA, 15 + (c + 1) * CA)
                P.dma("sync", sl_xo[0], [(xr[:, :, :CA], xT_own_v[:, :, cs])], writes=["xr"])
                op("vector", lambda e: e.tensor_scalar_mul(out=xr[:, :, :CA], in0=xr[:, :, :CA], scalar1=ALPHA), reads=["xr"], writes=["xr"])
                items = [(j, hd, gi) for j in range(4) for hd in range(2) for gi in range(len(groups))]

                def qk(idx, c=c, cs=cs, items=items):
                    j, hd, gi = items[idx]
                    sb = idx % NSB
                    for i2, kt in enumerate(groups[gi]):
                        bank = 2 + TPS * sb + i2
                        op("tensor", lambda e, j=j, hd=hd, kt=kt, bank=bank: e.matmul(
                            ps[:, bank, :CA], lhsT=kT[:, kt * 128:(kt + 1) * 128], rhs=qTp[:, 2 * j + hd, cs],
                            start=True, stop=True),
                           reads=["kT", "qT"], writes=[pk(bank)])

                for i in range(LA):
                    qk(i)
                deferred = []
                for n_c, fn_c in enumerate(carry):
                    deferred.append((3 + 3 * n_c, fn_c))
                carry = []
                for idx in range(len(items)):
                    j, hd, gi = items[idx]
                    tiles = groups[gi]
                    nt = len(tiles)
                    sb = idx % NSB
                    b0 = 2 + TPS * sb
                    op("scalar", lambda e, sb=sb, b0=b0, nt=nt: e.activation(out=pT[sb][:, 0:nt, :CA], in_=ps[:, b0:b0 + nt, :CA],
                                                                             func=AF.Exp, scale=0.125),
                       reads=[pk(b0 + i2) for i2 in range(nt)], writes=[f"pT{sb}"])
                    while deferred and deferred[0][0] <= idx:
                        deferred.pop(0)[1]()
                    if idx + LA < len(items):
                        qk(idx + LA)
                    for i2, kt in enumerate(tiles):
                        op("tensor", lambda e, hd=hd, kt=kt, sb=sb, i2=i2: e.matmul(
                            ps[:, hd, :CA], lhsT=Vx[:, kt, hd * 64:hd * 64 + 128], rhs=pT[sb][:, i2, :CA], start=(kt == 0), stop=(kt == NKT - 1)),
                           reads=["Vx", f"pT{sb}"], writes=[pk(hd)])
                    if hd == 1 and gi == len(groups) - 1:
                        op("vector", lambda e: e.reciprocal(out=rec[64:128, :CA], in_=ps[64:128, 0, :CA]), reads=[pk(0)], writes=["rec"])
                        op("vector", lambda e: e.reciprocal(out=rec[0:64, :CA], in_=ps[0:64, 1, :CA]), reads=[pk(1)], writes=["rec"])
                        op("vector", lambda e: e.tensor_tensor(out=On[0:64, :CA], in0=ps[0:64, 0, :CA], in1=rec[64:128, :CA], op=ALU.mult),
                           reads=[pk(0), "rec"], writes=["On"])
                        op("vector", lambda e: e.tensor_tensor(out=On[64:128, :CA], in0=ps[64:128, 1, :CA], in1=rec[0:64, :CA], op=ALU.mult),
                           reads=[pk(1), "rec"], writes=["On"])
                        op("vector", lambda e: e.tensor_tensor(out=r_sq[:, :CA], in0=On[:, :CA], in1=On[:, :CA], op=ALU.mult),
                           reads=["On"], writes=["r_sq"])
                        fb = b0
                        op("tensor", lambda e, fb=fb: e.matmul(ps[:, fb, :CA], lhsT=blk, rhs=r_sq[:, :CA], start=True, stop=True),
                           reads=["r_sq", "consts"], writes=[pk(fb)])
                        op("vector", lambda e, fb=fb: e.tensor_scalar_add(out=r_v[:, :CA], in0=ps[:, fb, :CA], scalar1=EPS),
                           reads=[pk(fb)], writes=["r_v"])

                        def norm_tail(j=j, cs=cs):
                            op("scalar", lambda e: e.activation(out=r_v[:, :CA], in_=r_v[:, :CA], func=AF.Ln), reads=["r_v"], writes=["r_v"])
                            op("scalar", lambda e: e.activation(out=r_r[:, :CA], in_=r_v[:, :CA], func=AF.Exp, scale=-0.5),
                               reads=["r_v"], writes=["r_r"])
                            op("vector", lambda e, j=j, cs=cs: e.scalar_tensor_tensor(
                                out=AT[:, j, cs], in0=On[:, :CA], scalar=vecs[:, V_AOG + j:V_AOG + j + 1], in1=r_r[:, :CA],
                                op0=ALU.mult, op1=ALU.mult),
                               reads=["On", "vecs", "r_r"], writes=["AT"])
                        deferred.append((idx + 2, norm_tail))
                while deferred:
                    deferred.pop(0)[1]()
                for f in range(8):
                    bank = 2 + (f % 2)
                    for k in range(8):
                        op("tensor", lambda e, f=f, k=k, bank=bank, cs=cs: e.matmul(
                            ps[:, bank, :CA], lhsT=wo[:, k, f * 128:(f + 1) * 128], rhs=AT[:, k, cs], start=(k == 0), stop=(k == 7)),
                           reads=["wo", "AT"], writes=[pk(bank)])
                    op("vector", lambda e, f=f, bank=bank: e.scalar_tensor_tensor(out=yb[:, f, :CA], in0=ps[:, bank, :CA], scalar=mcol(M_G1 + f),
                                                                                  in1=xr[:, f, :CA], op0=ALU.mult, op1=ALU.add),
                       reads=[pk(bank), "mod", "xr"], writes=[f"yb{f}"])
                srcs = [(yb[:, f, :CA], f"yb{f}") for f in range(8)]
                ln_stats(srcs, CA, ones1024, "ones1024", 4, 5)

                def ln1_apply_f(f, cs=cs):
                    i = cnt() % 2
                    op("vector", lambda e, f=f, i=i: e.tensor_tensor(out=tA[i][:, :CA], in0=yb[:, f, :CA], in1=s_mean[:, :CA], op=ALU.subtract),
                       reads=[f"yb{f}", "s_mean"], writes=[f"tA{i}"])
                    op("vector", lambda e, i=i: e.tensor_tensor(out=tB[i][:, :CA], in0=tA[i][:, :CA], in1=s_rstd[:, :CA], op=ALU.mult),
                       reads=[f"tA{i}", "s_rstd"], writes=[f"tB{i}"])
                    op("vector", lambda e, f=f, i=i, cs=cs: e.tensor_scalar(
                        out=X1[:, f, cs], in0=tB[i][:, :CA], scalar1=vecs[:, V_LN1G + f:V_LN1G + f + 1], scalar2=vecs[:, V_LN1B + f:V_LN1B + f + 1],
                        op0=ALU.mult, op1=ALU.add),
                       reads=[f"tB{i}", "vecs"], writes=["X1"])

                if c + 1 < NCH:
                    carry = [(lambda f=f, fn=ln1_apply_f: fn(f)) for f in range(8)]
                else:
                    for f in range(8):
                        ln1_apply_f(f)
            if debug and h == 0:
                P.dma("sync", sl_dbg, [(dbg, X1)], reads=["X1"])
            P.barrier()

            AR.top = hmark
            mark = AR.top
            u2 = AR.alloc([8, 514], BF16)
            GT = AR.alloc([NFC, 512], BF16)
            y2 = AR.alloc([8, 512], F32)
            Pb = [AR.alloc([514], BF16) for _ in range(2)]
            Wu = [AR.alloc([8, 128], BF16) for _ in range(4)]
            Wd = [AR.alloc([NFC, 128], BF16) for _ in range(3)]
            d3 = [AR.alloc([3, 128], BF16) for _ in range(2)]
            gl = [r_t1, r_t2]
            ob = [AR.alloc([512], F32) for _ in range(3)]
            g2t = [r_sq, r_v]
            units = [(cc, which) for cc in range(NFC) for which in range(2)]
            NU = len(units)
            pending_ln2 = []
            for gq in range(2):
                w0 = 15 + gq * 512

                def wload(u):
                    cc, which = units[u]
                    wb = u % 4
                    P.dma("gpsimd", sl_wu[wb], [(Wu[wb], wup_d[cc + NFC * which])], writes=[f"Wu{wb}"])

                def dload(f):
                    db = f % 3
                    P.dma("gpsimd", sl_wd[db], [(Wd[db], wdn_d[f])], writes=[f"Wd{db}"])

                def group_head(g):
                    wload(0)
                    wload(1)
                    wg0 = 15 + g * 512
                    for wc in range(2):
                        ws = slice(wg0 + wc * 257, wg0 + (wc + 1) * 257)
                        srcs_u = [(X1[:, k, ws], "X1") for k in range(8)]
                        ln_stats(srcs_u, 257, ones1024, "ones1024", 0, 1)
                        ln_apply(srcs_u, [(u2[:, k, wc * 257:(wc + 1) * 257], "u2") for k in range(8)], 257,
                                 [mcol(M_SC2P + k) for k in range(8)], [mcol(M_SH2 + k) for k in range(8)], ["mod"])

                if gq == 0:
                    group_head(0)

                def ffn_P(u, gq=gq, mL=mL, mR=mR):
                    cc, which = units[u]
                    chunk = cc + NFC * which
                    pb = u % 2
                    wb = u % 4
                    if u + 2 < NU:
                        wload(u + 2)
                    for j in range(3):
                        col = V_FW + j * 44 + chunk
                        op("vector", lambda e, pb=pb, j=j, col=col: e.tensor_scalar_mul(
                            out=d3[pb][:, j, :], in0=ident, scalar1=vecs[:, col:col + 1]),
                           reads=["ident", "vecs"], writes=[f"d3_{pb}"])
                    for wc in range(2):
                        bank = 2 + 2 * pb + wc
                        for kc in range(8):
                            op("tensor", lambda e, wb=wb, kc=kc, wc=wc, bank=bank: e.matmul(
                                ps[:, bank, :257], lhsT=Wu[wb][:, kc, :], rhs=u2[:, kc, wc * 257:(wc + 1) * 257], start=(kc == 0), stop=(kc == 7)),
                               reads=[f"Wu{wb}", "u2"], writes=[pk(bank)])
                        if wc == 0:
                            op("scalar", lambda e, pb=pb, bank=bank: e.activation(out=Pb[pb][:, 0:257], in_=ps[:, bank, :257], func=AF.Identity),
                               reads=[pk(bank)], writes=[f"Pb{pb}"])
                        else:
                            op("vector", lambda e, pb=pb, bank=bank: e.tensor_copy(out=Pb[pb][:, 257:514], in_=ps[:, bank, :257]),
                               reads=[pk(bank)], writes=[f"Pb{pb}"])
                    if gq == 0:
                        op("vector", lambda e, pb=pb, mL=mL: e.tensor_scalar_mul(out=Pb[pb][:, 0:1], in0=Pb[pb][:, 0:1], scalar1=mL),
                           reads=[f"Pb{pb}", "vecs"], writes=[f"Pb{pb}"])
                    else:
                        op("vector", lambda e, pb=pb, mR=mR: e.tensor_scalar_mul(out=Pb[pb][:, 513:514], in0=Pb[pb][:, 513:514], scalar1=mR),
                           reads=[f"Pb{pb}", "vecs"], writes=[f"Pb{pb}"])

                def ffn_C(u):
                    cc, which = units[u]
                    pb = u % 2
                    cb = 6 + which
                    for j in range(3):
                        op("tensor", lambda e, pb=pb, j=j, cb=cb: e.matmul(
                            ps[:, cb, :], lhsT=d3[pb][:, j, :], rhs=Pb[pb][:, j:j + 512], start=(j == 0), stop=(j == 2)),
                           reads=[f"d3_{pb}", f"Pb{pb}"], writes=[pk(cb)])
                    if which == 1:
                        gi = cc % 2
                        op("scalar", lambda e, gi=gi, cc=cc: e.activation(out=gl[gi], in_=ps[:, 7, :], func=AF.Gelu,
                                                                          bias=vecs[:, V_FB + NFC + cc:V_FB + NFC + cc + 1]),
                           reads=[pk(7), "vecs"], writes=[f"gl{gi}"])
                        op("vector", lambda e, gi=gi, cc=cc: e.scalar_tensor_tensor(
                            out=GT[:, cc, :], in0=ps[:, 6, :], scalar=vecs[:, V_FB + cc:V_FB + cc + 1], in1=gl[gi], op0=ALU.add, op1=ALU.mult),
                           reads=[pk(6), "vecs", f"gl{gi}"], writes=[f"GT{cc}"])

                ffn_P(0)
                for u in range(NU):
                    if u + 1 < NU:
                        ffn_P(u + 1)
                    if u in (4, 12, 20):
                        dload((u - 4) // 8)
                    ffn_C(u)
                    if pending_ln2 and u >= 2 and u % 2 == 0:
                        pending_ln2.pop(0)()
                while pending_ln2:
                    pending_ln2.pop(0)()
                xs_g = slice(16 + gq * 512, 16 + (gq + 1) * 512)
                op("vector", lambda e, xs_g=xs_g: e.tensor_scalar_mul(out=y2[:, :, :], in0=X1[:, :, xs_g], scalar1=ALPHA),
                   reads=["X1"], writes=[f"y2_{f}" for f in range(8)])
                if gq == 0:
                    group_head(1)
                xs = slice(16 + gq * 512, 16 + (gq + 1) * 512)
                for f in range(8):
                    db = f % 2
                    wdb = f % 3
                    bank = 2 + (f % 2)
                    for cc in range(NFC):
                        op("tensor", lambda e, wdb=wdb, cc=cc, bank=bank: e.matmul(
                            ps[:, bank, :], lhsT=Wd[wdb][:, cc, :], rhs=GT[:, cc, :], start=(cc == 0), stop=(cc == NFC - 1)),
                           reads=[f"Wd{wdb}", f"GT{cc}"], writes=[pk(bank)])
                    if f + 3 < 8:
                        dload(f + 3)
                    op("vector", lambda e, f=f, bank=bank: e.scalar_tensor_tensor(out=y2[:, f, :], in0=ps[:, bank, :], scalar=mcol(M_G2 + f),
                                                                                  in1=y2[:, f, :], op0=ALU.mult, op1=ALU.add),
                       reads=[pk(bank), "mod", f"y2_{f}"], writes=[f"y2_{f}"])
                srcs = [(y2[:, f, :], f"y2_{f}") for f in range(8)]
                t0 = h * HALF + gq * 512

                def ln2_out(f, srcs=srcs, t0=t0):
                    oi = f % 3
                    ln_apply([srcs[f]], [(ob[oi], f"ob{oi}")], 512, [vecs[:, V_LN2G + f:V_LN2G + f + 1]], [vecs[:, V_LN2B + f:V_LN2B + f + 1]], ["vecs"])
                    P.dma("sync", sl_out[oi], [(outT_v[:, f, t0:t0 + 512], ob[oi])], reads=[f"ob{oi}"])

                if gq == 1:
                    ln_stats(srcs, 512, ones1024, "ones1024", 4, 5)
                    for f in range(8):
                        ln2_out(f)
                else:
                    idxs = {}

                    def sl_E(k, srcs=srcs, idxs=idxs):
                        idxs[k] = ln_stats_E(srcs[k][0], srcs[k][1], 512)

                    def sl_M(k, idxs=idxs):
                        ln_stats_M(idxs[k], k, 8, 512, ones1024, "ones1024", 0, 1)

                    pending_ln2.append(lambda: sl_E(0))
                    for k in range(1, 8):
                        pending_ln2.append(lambda k=k: (sl_M(k - 1), sl_E(k)))
                    pending_ln2.append(lambda: sl_M(7))
                    pending_ln2.append(lambda: ln_stats_fin(512, 0, 1))
                    for f in range(8):
                        pending_ln2.append(lambda f=f, fn=ln2_out: fn(f))
            P.barrier()
            AR.top = hmark

        for sl in P.all_slots:
            if sl.count > 0:
                P.prog["sync"].append(lambda e, s=sl.sem, v=sl.count: e.wait_ge(s, v))
        P.emit()
    return nc


_PROG_CACHE = {}


def _rope_tables(tok):
    tok = np.asarray(tok)
    inv = (10000.0 ** (-np.arange(16, dtype=np.float32) / 16.0)).astype(np.float32)
    rows = (tok // 64).astype(np.float32)
    cols = (tok % 64).astype(np.float32)
    ang_r = rows[None, :] * inv[:, None]
    ang_c = cols[None, :] * inv[:, None]
    C = np.zeros((64, len(tok)), np.float32)
    S = np.zeros((64, len(tok)), np.float32)
    C[0:16] = np.cos(ang_r); C[16:32] = np.cos(ang_r); C[32:48] = np.cos(ang_c); C[48:64] = np.cos(ang_c)
    S[0:16] = -np.sin(ang_r); S[16:32] = np.sin(ang_r); S[32:48] = -np.sin(ang_c); S[48:64] = np.sin(ang_c)
    return np.concatenate([C, C], 0), np.concatenate([S, S], 0)


def _partner():
    d = np.arange(64)
    return np.where((d % 32) < 16, d + 16, d - 16)


def _col(v):
    v = np.asarray(v, np.float32)
    return np.ascontiguousarray(v.reshape(-1, 128).T)


def kernel(x, c, w_ada, b_ada, w_in, q_norm_g, k_norm_g, conv_dw_w, conv_dw_b, conv_ln_g, conv_ln_b, w_conv_pw2,
           attn_out_g, conv_out_g, w_o, ln1_g, ln1_b, w_up, ffn_dw_w, ffn_dw_b, w_down, ln2_g, ln2_b, _debug=False):
    f32 = np.float32
    x = np.asarray(x, f32); c = np.asarray(c, f32)
    w_ada = np.asarray(w_ada, f32)[0]; b_ada = np.asarray(b_ada, f32)[0]; w_in = np.asarray(w_in, f32)[0]
    qg = np.asarray(q_norm_g, f32)[0]; kg = np.asarray(k_norm_g, f32)[0]
    dw_w = np.asarray(conv_dw_w, f32)[0]; dw_b = np.asarray(conv_dw_b, f32)[0]
    cln_g = np.asarray(conv_ln_g, f32)[0]; cln_b = np.asarray(conv_ln_b, f32)[0]
    pw2 = np.asarray(w_conv_pw2, f32)[0]; aog = np.asarray(attn_out_g, f32)[0]; cog = np.asarray(conv_out_g, f32)[0]
    w_o = np.asarray(w_o, f32)[0]; w_up = np.asarray(w_up, f32)[0]; w_down = np.asarray(w_down, f32)[0]
    fw = np.asarray(ffn_dw_w, f32)[0]; fb = np.asarray(ffn_dw_b, f32)[0]
    l1g = np.asarray(ln1_g, f32)[0]; l1b = np.asarray(ln1_b, f32)[0]; l2g = np.asarray(ln2_g, f32)[0]; l2b = np.asarray(ln2_b, f32)[0]

    part = _partner()
    wq = w_in[:, 0:512].reshape(D, 8, 64)
    wk = w_in[:, 512:640].reshape(D, 2, 64)
    wv = w_in[:, 640:768]
    wglu = w_in[:, 768:1792]
    head_order = [0, 4, 1, 5, 2, 6, 3, 7]
    wq_p = wq[:, head_order, :].reshape(D, 512)
    wq_sw = wq[:, head_order, :][:, :, part].reshape(D, 512)
    wk_p = wk.reshape(D, 128)
    wk_sw = wk[:, :, part].reshape(D, 128)
    w_in_cat = np.concatenate([wq_p, wq_sw, wglu, wk_p, wk_sw, wv], axis=1)
    w_in_p = np.ascontiguousarray(w_in_cat.reshape(8, 128, 2432).transpose(1, 0, 2))
    w_ada_p = np.ascontiguousarray(w_ada.reshape(8, 128, 6, 1024).transpose(2, 1, 0, 3))
    w_pw2_p = np.ascontiguousarray(pw2.reshape(4, 128, 512).transpose(1, 0, 2))
    rows = np.concatenate([np.concatenate([np.arange(j * 64, (j + 1) * 64), np.arange((4 + j) * 64, (5 + j) * 64)]) for j in range(4)]
                          + [np.arange(512, 1024)])
    w_o_p = np.ascontiguousarray(w_o[rows, :].reshape(8, 128, 1024).transpose(1, 0, 2))
    w_up_p = np.ascontiguousarray(w_up.reshape(8, 128, 44, 128).transpose(2, 1, 0, 3))
    w_down_p = np.ascontiguousarray(w_down.reshape(NFC, 128, 8, 128).transpose(2, 1, 0, 3))

    consts = np.zeros((128, 256), f32)
    consts[:, 0:128] = np.eye(128, dtype=f32)
    consts[0:64, 128:192] = 1.0 / 64.0
    consts[64:128, 192:256] = 1.0 / 64.0

    ropekC, ropekS = _rope_tables(np.arange(SEQ))
    ropek = np.ascontiguousarray(np.stack([ropekC, ropekS]))

    def vec_common():
        v = np.zeros((128, NV), f32)
        v[:, V_BADA:V_BADA + 48] = _col(b_ada)
        g2 = np.concatenate([qg, qg]); v[:, V_QG] = g2; v[:, V_QG + 1] = np.concatenate([qg[part], qg[part]])
        k2 = np.concatenate([kg, kg]); v[:, V_KG] = k2; v[:, V_KG + 1] = np.concatenate([kg[part], kg[part]])
        v[:, V_DWB:V_DWB + 4] = _col(dw_b)
        v[:, V_CLNG:V_CLNG + 4] = _col(cln_g)
        v[:, V_CLNB:V_CLNB + 4] = _col(cln_b)
        v[:, V_COG:V_COG + 4] = _col(cog.reshape(-1))
        for j in range(4):
            v[0:64, V_AOG + j] = aog[j]
            v[64:128, V_AOG + j] = aog[4 + j]
        v[:, V_LN1G:V_LN1G + 8] = _col(l1g); v[:, V_LN1B:V_LN1B + 8] = _col(l1b)
        v[:, V_LN2G:V_LN2G + 8] = _col(l2g); v[:, V_LN2B:V_LN2B + 8] = _col(l2b)
        v[:, V_FB:V_FB + 44] = _col(fb)
        for j in range(3):
            v[:, V_FW + j * 44:V_FW + (j + 1) * 44] = _col(fw[j])
        for i in range(4):
            v[:, V_DW + i * 31:V_DW + (i + 1) * 31] = dw_w[:, i * 128:(i + 1) * 128].T
        return v

    vcommon = vec_common()
    in_maps = []
    for ci in range(NCORE):
        b, r = divmod(ci, 4)
        start = r * OWN
        xb_T = np.ascontiguousarray(x[b].T)
        xown = np.zeros((2, D, HT), f32)
        rq = np.zeros((2, 2, 128, HT), f32)
        v = vcommon.copy()
        v[:, V_C:V_C + 8] = _col(c[b])
        for h in range(2):
            t0 = start + h * HALF - 16
            tok = np.arange(t0, t0 + HT)
            valid = (tok >= 0) & (tok < SEQ)
            xown[h][:, valid] = xb_T[:, tok[valid]]
            Cq, Sq = _rope_tables(np.clip(tok, 0, SEQ - 1))
            rq[h, 0] = Cq; rq[h, 1] = Sq
            v[:, V_MASK + 2 * h] = 1.0 if t0 + 15 >= 0 else 0.0
            v[:, V_MASK + 2 * h + 1] = 1.0 if t0 + 16 + HALF < SEQ else 0.0
        in_maps.append({
            "xT_all": xb_T, "xT_own": xown, "vecs": v, "w_ada_p": w_ada_p, "w_in_p": w_in_p, "ropeq": rq, "ropek": ropek,
            "consts": consts, "w_pw2_p": w_pw2_p, "w_o_p": w_o_p, "w_up_p": w_up_p, "w_down_p": w_down_p,
        })

    key = bool(_debug)
    if key not in _PROG_CACHE:
        _PROG_CACHE[key] = build_program(debug=key)
    nc = _PROG_CACHE[key]
    res = run_bass_kernel_spmd(nc, in_maps, core_ids=list(range(NCORE)))
    out = np.empty((NB, SEQ, D), f32)
    for ci in range(NCORE):
        b, r = divmod(ci, 4)
        out[b, r * OWN:(r + 1) * OWN, :] = res.results[ci]["outT"].T
    if _debug:
        return out, [res.results[ci]["dbg"] for ci in range(NCORE)]
    return out
```
